# Optimizing a Trainium2 kernel written in Bass

```python
import jax, jax.numpy as jnp
from jax import lax
import numpy as np

D_MODEL = 1024
BATCH = 4
SEQ = 8192
DEPTH = 2

CHUNK = 64
N_META = 16
MIX = D_MODEL
CONV_WIDTH_CH = MIX // 2
CONV_HEADS = 8
CONV_HEAD_DIM = CONV_WIDTH_CH // CONV_HEADS
CONV_K = 31
POOL_WIDTH_CH = MIX - CONV_WIDTH_CH
POOL_WINDOWS = (2, 4, 8, 16)
POOL_GROUPS = len(POOL_WINDOWS)
POOL_GROUP_DIM = POOL_WIDTH_CH // POOL_GROUPS
IN_COLS = 2 * CONV_WIDTH_CH + POOL_WIDTH_CH
D_FF = 2816
FFN_CONV_K = 3
EPS = 1e-6

kernel_name = "hybrid_conformer_conv_pool_encoder"


def rmsnorm(x, g):
    xf = x.astype(jnp.float32)
    y = xf * lax.rsqrt(jnp.mean(xf * xf, axis=-1, keepdims=True) + EPS)
    return (y * g.astype(jnp.float32)).astype(x.dtype)


def causal_dwconv(x, k):
    w, c = k.shape
    xp = jnp.pad(x, ((0, 0), (w - 1, 0), (0, 0)))
    return lax.conv_general_dilated(
        xp, k[:, None, :].astype(x.dtype), window_strides=(1,), padding="VALID",
        dimension_numbers=("NWC", "WIO", "NWC"), feature_group_count=c)


def conformer_conv_group(a, g, dw_k, dw_b, ln_g, ln_b):
    u = a * jax.nn.sigmoid(g)
    u = causal_dwconv(u, dw_k) + dw_b.astype(a.dtype)
    bsz, length, c = u.shape
    uh = u.reshape(bsz, length, CONV_HEADS, CONV_HEAD_DIM).astype(jnp.float32)
    mu = jnp.mean(uh, axis=-1, keepdims=True)
    var = jnp.mean(jnp.square(uh - mu), axis=-1, keepdims=True)
    uh = (uh - mu) * lax.rsqrt(var + EPS)
    u = uh.reshape(bsz, length, c) * ln_g.astype(jnp.float32) + ln_b.astype(jnp.float32)
    return jax.nn.silu(u).astype(a.dtype)


def multiscale_pool_group(p, pool_w, pool_scale):
    bsz, length, c = p.shape
    pf = p.astype(jnp.float32)
    cs = jnp.pad(jnp.cumsum(pf, axis=1), ((0, 0), (1, 0), (0, 0)))
    t = jnp.arange(length)
    outs = []
    for gi, w in enumerate(POOL_WINDOWS):
        sl = slice(gi * POOL_GROUP_DIM, (gi + 1) * POOL_GROUP_DIM)
        cg = cs[:, :, sl]
        upper = cg[:, 1:]
        lower = jnp.pad(cg[:, :length + 1 - w], ((0, 0), (w - 1, 0), (0, 0)))
        cnt = jnp.minimum(t + 1, w).astype(jnp.float32)[None, :, None]
        outs.append((upper - lower) / cnt - pf[:, :, sl])
    d = jnp.stack(outs, axis=2).astype(p.dtype)
    y = jnp.einsum("blgc,gcd->blgd", d, pool_w).reshape(bsz, length, c)
    return y * pool_scale


def setup_inputs(seed: int = 0) -> dict:
    key = jax.random.key(seed)
    ks = jax.random.split(key, 16)
    f32 = jnp.float32
    nrm = lambda k, shape, s: (jax.random.normal(k, shape, f32) * s)
    return {
        "x": nrm(ks[0], (BATCH, SEQ, D_MODEL), 1.0),
        "meta_tokens": nrm(ks[1], (N_META, D_MODEL), 1.0),
        "norm1_g": 1.0 + nrm(ks[2], (DEPTH, D_MODEL), 0.02),
        "w_in": nrm(ks[3], (DEPTH, D_MODEL, IN_COLS), D_MODEL ** -0.5),
        "conv_dw_k": nrm(ks[4], (DEPTH, CONV_K, CONV_WIDTH_CH), CONV_K ** -0.5),
        "conv_dw_b": nrm(ks[5], (DEPTH, CONV_WIDTH_CH), 0.02),
        "conv_ln_g": 1.0 + nrm(ks[6], (DEPTH, CONV_WIDTH_CH), 0.02),
        "conv_ln_b": nrm(ks[7], (DEPTH, CONV_WIDTH_CH), 0.02),
        "pool_w": nrm(ks[8], (DEPTH, POOL_GROUPS, POOL_GROUP_DIM, POOL_GROUP_DIM), POOL_GROUP_DIM ** -0.5),
        "pool_scale": 1.0 + nrm(ks[9], (DEPTH, POOL_WIDTH_CH), 0.02),
        "w_out": nrm(ks[10], (DEPTH, MIX, D_MODEL), MIX ** -0.5),
        "norm2_g": 1.0 + nrm(ks[11], (DEPTH, D_MODEL), 0.02),
        "w_up": nrm(ks[12], (DEPTH, D_MODEL, 2 * D_FF), D_MODEL ** -0.5),
        "ffn_dw_k": nrm(ks[13], (DEPTH, FFN_CONV_K, 2 * D_FF), FFN_CONV_K ** -0.5),
        "w_down": nrm(ks[14], (DEPTH, D_FF, D_MODEL), D_FF ** -0.5),
        "final_g": 1.0 + nrm(ks[15], (D_MODEL,), 0.02),
    }


def reference(x, meta_tokens, norm1_g, w_in, conv_dw_k, conv_dw_b, conv_ln_g, conv_ln_b,
              pool_w, pool_scale, w_out, norm2_g, w_up, ffn_dw_k, w_down, final_g):
    bsz = x.shape[0]
    meta = jnp.broadcast_to(meta_tokens[None].astype(x.dtype), (bsz, N_META, D_MODEL))
    h = jnp.concatenate([meta, x], axis=1)
    for i in range(DEPTH):
        hn = rmsnorm(h, norm1_g[i])
        z = hn @ w_in[i]
        a = z[..., :CONV_WIDTH_CH]
        g = z[..., CONV_WIDTH_CH:2 * CONV_WIDTH_CH]
        p = z[..., 2 * CONV_WIDTH_CH:]
        y_conv = conformer_conv_group(a, g, conv_dw_k[i], conv_dw_b[i], conv_ln_g[i], conv_ln_b[i])
        y_pool = multiscale_pool_group(p, pool_w[i], pool_scale[i])
        h = h + jnp.concatenate([y_conv, y_pool], axis=-1) @ w_out[i]
        hn = rmsnorm(h, norm2_g[i])
        ug = causal_dwconv(hn @ w_up[i], ffn_dw_k[i])
        gate, val = ug[..., :D_FF], ug[..., D_FF:]
        h = h + (jax.nn.silu(gate) * val) @ w_down[i]
    return rmsnorm(h, final_g)[:, N_META:]
```

```python
import numpy as np
from contextlib import ExitStack
import concourse.bass as bass
import concourse.mybir as mybir
from concourse.bass_utils import run_bass_kernel_spmd

F32 = mybir.dt.float32
BF16 = mybir.dt.bfloat16
AF = mybir.ActivationFunctionType
ALU = mybir.AluOpType

D_MODEL = 1024
N_META = 16
D_FF = 2816
NPAIR = D_FF // 128
CONV_K = 31
EPS = 1e-6
N = 424
HALO = 68
NSLOT = 6
SLOT_ELEMS = 2816

MIX_CHUNKS = [("win", 6, 2048), ("pw", 1, 512), ("wout", 4, 2048)]
FFN_CHUNKS = [("wup", NPAIR, 2048), ("wdn", 8, 2816)]
LAYER_ELEMS = sum(n * e for _, n, e in MIX_CHUNKS + FFN_CHUNKS)

V_N1G, V_N2G, V_CB, V_LNG, V_LNB, V_PSC, V_CW, V_FK = 0, 8, 16, 20, 24, 28, 32, 156
V_LAYER = 288


class Op:
    __slots__ = ("eng", "emit", "deps", "signal", "sigcount", "is_dma", "dsem", "dval", "idx")


class Sched:
    COMPUTE = ("pe", "act", "dve", "pool")

    def __init__(self):
        self.ops = []
        self.last_writer = {}
        self.readers = {}
        self.last_on = {}
        self.arena_prev = []
        self.arena_cur = {}

    def add(self, eng, emit, reads=(), writes=(), dma=None, arena=False):
        op = Op()
        op.eng = eng
        op.emit = emit
        op.signal = False
        op.sigcount = 0
        op.is_dma = dma is not None
        op.dsem, op.dval = dma if dma is not None else (None, None)
        op.idx = len(self.ops)
        deps = {}

        def need(d, kind):
            if d is op:
                return
            if (not d.is_dma) and (not op.is_dma) and d.eng == op.eng:
                if d.eng == "pe":
                    return
                if kind != "RAW":
                    return
            deps[d.idx] = d

        for k in reads:
            w = self.last_writer.get(k)
            if w is not None:
                need(w, "RAW")
        for k in writes:
            w = self.last_writer.get(k)
            if w is not None:
                need(w, "WAW")
            for r in self.readers.get(k, {}).values():
                need(r, "WAR")
        if arena:
            for d in self.arena_prev:
                need(d, "WAR")
            self.arena_cur[("dma", op.idx) if op.is_dma else op.eng] = op
        for k in writes:
            self.last_writer[k] = op
            self.readers[k] = {}
        for k in reads:
            rk = ("dma", op.idx) if op.is_dma else op.eng
            self.readers.setdefault(k, {})[rk] = op
        op.deps = list(deps.values())
        for d in op.deps:
            if not d.is_dma:
                d.signal = True
        self.ops.append(op)
        self.last_on[op.eng] = op
        return op

    def arena_handoff(self):
        self.arena_prev = list(self.arena_cur.values())
        self.arena_cur = {}

    def emit_all(self, nc, block_fns, sems):
        cnt = {e: 0 for e in self.COMPUTE}
        for op in self.ops:
            if not op.is_dma and op.signal:
                cnt[op.eng] += 1
                op.sigcount = cnt[op.eng]
        per_eng = {}
        for op in self.ops:
            per_eng.setdefault(op.eng, []).append(op)

        def token(d):
            if d.is_dma:
                return d.dsem, d.dval
            return sems[d.eng], d.sigcount

        def make(engname, oplist):
            def body(e):
                waited = {}
                for op in oplist:
                    for d in op.deps:
                        s, v = token(d)
                        key = id(s)
                        if waited.get(key, 0) >= v:
                            continue
                        e.wait_ge(s, v)
                        waited[key] = v
                    ins = op.emit(e)
                    if op.is_dma:
                        ins.then_inc(op.dsem, 16)
                    elif op.signal:
                        ins.then_inc(sems[op.eng], 1)
            return body

        for engname, oplist in per_eng.items():
            block_fns[engname](make(engname, oplist))


def build_program(NT, NSEG, DEPTH, debug_h=False):
    TSEG = NT * N
    OWN = TSEG - HALO
    nc = bass.Bass("TRN2", target_bir_lowering=False)
    xin = nc.dram_tensor("xin", [NSEG, NT, 128, 8 * N], F32, kind="ExternalInput").ap()
    wf = nc.dram_tensor("wf", [DEPTH, 128 * LAYER_ELEMS], F32, kind="ExternalInput").ap()
    vecs_d = nc.dram_tensor("vecs", [128, DEPTH * V_LAYER + 8], F32, kind="ExternalInput").ap()
    consts_d = nc.dram_tensor("consts", [128, 4 * 128], F32, kind="ExternalInput").ap()
    mask_d = nc.dram_tensor("mask", [128, NSEG * HALO], F32, kind="ExternalInput").ap()
    pcnt_d = nc.dram_tensor("pcnt", [128, NSEG * 4 * 16], F32, kind="ExternalInput").ap()
    out_d = nc.dram_tensor("out", [NSEG, NT, 128, 8 * N], F32, kind="ExternalOutput").ap()
    wbf = nc.dram_tensor("wbf", [DEPTH, 128 * LAYER_ELEMS], BF16).ap()

    S = Sched()
    ARENA_BYTES = max(4 * CONV_K * 128 * 2, NPAIR * N * 2 + 8 * N * 4)
    with ExitStack() as es:
        H = es.enter_context(nc.sbuf_tensor("H", [128, 8, TSEG], F32))
        ARENA = es.enter_context(nc.sbuf_tensor("ARENA", [128, ARENA_BYTES // 2], BF16))
        U = es.enter_context(nc.sbuf_tensor("U", [128, 2, 4, N + 30], BF16))
        HN = es.enter_context(nc.sbuf_tensor("HN", [128, 8, N + 2], BF16))
        RSTD = es.enter_context(nc.sbuf_tensor("RSTD", [128, N], F32))
        SQ = es.enter_context(nc.sbuf_tensor("SQ", [128, 8, N], BF16))
        TH = es.enter_context(nc.sbuf_tensor("TH", [128, 2, N], F32))
        CF = es.enter_context(nc.sbuf_tensor("CF", [128, 4, N], F32))
        CB = es.enter_context(nc.sbuf_tensor("CB", [128, 4, N], BF16))
        C2 = es.enter_context(nc.sbuf_tensor("C2", [128, 4, N], BF16))
        RS = es.enter_context(nc.sbuf_tensor("RS", [128, 2, N], F32))
        Y = es.enter_context(nc.sbuf_tensor("Y", [128, 8, N], BF16))
        P = es.enter_context(nc.sbuf_tensor("P", [128, 4, N + 16], F32))
        PA4 = es.enter_context(nc.sbuf_tensor("PA4", [128, 4, N + 16], F32))
        PB4 = es.enter_context(nc.sbuf_tensor("PB4", [128, 3, N + 16], F32))
        PT = es.enter_context(nc.sbuf_tensor("PT", [128, 4, 16], F32))
        D = es.enter_context(nc.sbuf_tensor("D", [128, 4, N], BF16))
        WR = es.enter_context(nc.sbuf_tensor("WR", [128, NSLOT, SLOT_ELEMS], BF16))
        IDENT = es.enter_context(nc.sbuf_tensor("IDENT", [128, 128], F32))
        ONES = es.enter_context(nc.sbuf_tensor("ONES", [128, 128], BF16))
        LNB = es.enter_context(nc.sbuf_tensor("LNB", [128, 128], BF16))
        NLNB = es.enter_context(nc.sbuf_tensor("NLNB", [128, 128], BF16))
        VEC = es.enter_context(nc.sbuf_tensor("VEC", [128, DEPTH * V_LAYER + 8], F32))
        MASK = es.enter_context(nc.sbuf_tensor("MASK", [128, NSEG, HALO], F32))
        PCNT = es.enter_context(nc.sbuf_tensor("PCNT", [128, NSEG, 4, 16], F32))
        EPSV = es.enter_context(nc.sbuf_tensor("EPSV", [128, 1], F32))
        PS = es.enter_context(nc.psum_tensor("PS", [128, 8, 512], F32))
        s_pe = es.enter_context(nc.semaphore("s_pe"))
        s_act = es.enter_context(nc.semaphore("s_act"))
        s_dve = es.enter_context(nc.semaphore("s_dve"))
        s_pool = es.enter_context(nc.semaphore("s_pool"))
        s_cst = es.enter_context(nc.semaphore("s_cst"))
        s_cstb = es.enter_context(nc.semaphore("s_cstb"))
        s_o0 = es.enter_context(nc.semaphore("s_o0"))
        s_o1 = es.enter_context(nc.semaphore("s_o1"))
        s_o2 = es.enter_context(nc.semaphore("s_o2"))
        s_o3 = es.enter_context(nc.semaphore("s_o3"))
        s_o4 = es.enter_context(nc.semaphore("s_o4"))
        s_castA = es.enter_context(nc.semaphore("s_castA"))
        s_castB = es.enter_context(nc.semaphore("s_castB"))
        s_castC = es.enter_context(nc.semaphore("s_castC"))
        s_castD = es.enter_context(nc.semaphore("s_castD"))
        s_castE = es.enter_context(nc.semaphore("s_castE"))
        s_castF = es.enter_context(nc.semaphore("s_castF"))
        s_w0 = es.enter_context(nc.semaphore("s_w0"))
        s_w1 = es.enter_context(nc.semaphore("s_w1"))
        s_w2 = es.enter_context(nc.semaphore("s_w2"))
        s_w3 = es.enter_context(nc.semaphore("s_w3"))
        s_w4 = es.enter_context(nc.semaphore("s_w4"))
        s_w5 = es.enter_context(nc.semaphore("s_w5"))
        s_x0 = es.enter_context(nc.semaphore("s_x0"))
        s_x1 = es.enter_context(nc.semaphore("s_x1"))
        s_x2 = es.enter_context(nc.semaphore("s_x2"))
        s_x3 = es.enter_context(nc.semaphore("s_x3"))
        s_x4 = es.enter_context(nc.semaphore("s_x4"))
        block = es.enter_context(nc.Block())
        s_w = [s_w0, s_w1, s_w2, s_w3, s_w4, s_w5]
        s_x = [s_x0, s_x1, s_x2, s_x3, s_x4]
        s_out = [s_o0, s_o1, s_o2, s_o3, s_o4]
        s_cast = [s_castA, s_castB, s_castC, s_castD, s_castE, s_castF]
        DG = ARENA[:, 0:4 * CONV_K * 128].rearrange("p (c k m) -> p c k m", c=4, k=CONV_K)
        ACTV = ARENA[:, 0:NPAIR * N].rearrange("p (j n) -> p j n", j=NPAIR)
        FT = ARENA[:, NPAIR * N:NPAIR * N + 8 * N * 2].bitcast(F32).rearrange("p (a n) -> p a n", a=8)

        n_cst = [0]

        def cst_load(dst, src, keys):
            n_cst[0] += 1
            S.add("sp", lambda e, dst=dst, src=src: e.dma_start(out=dst, in_=src), writes=keys, dma=(s_cst, None))

        S.add("dve", lambda e: e.memset(EPSV[:, :], EPS), writes=[("EPSV",)])
        cst_ops_start = len(S.ops)
        cst_load(VEC[:, :], vecs_d[:, :], [("VEC",)])
        cst_load(IDENT[:, :], consts_d[:, 0:128], [("IDENT",)])
        cst_load(MASK[:, :, :].rearrange("p s h -> p (s h)"), mask_d[:, :], [("MASK",)])
        cst_load(PCNT[:, :, :, :].rearrange("p s g t -> p (s g t)"), pcnt_d[:, :], [("PCNT",)])
        for op in S.ops[cst_ops_start:]:
            op.dval = 16 * n_cst[0]
        o1 = S.add("pool", lambda e: e.dma_start(out=ONES[:, :], in_=consts_d[:, 128:256]), writes=[("ONES",)], dma=(s_cstb, 32))
        o2 = S.add("pool", lambda e: e.dma_start(out=LNB[:, :], in_=consts_d[:, 256:384]), writes=[("LNB",)], dma=(s_cstb, 48))
        o3 = S.add("pool", lambda e: e.dma_start(out=NLNB[:, :], in_=consts_d[:, 384:512]), writes=[("NLNB",)], dma=(s_cstb, 48))
        o1.dval = 48

        chunk_tab = []
        off = 0
        for name, n, e in MIX_CHUNKS:
            for i in range(n):
                chunk_tab.append((off, e, "mix", name, i))
                off += 128 * e
        for name, n, e in FFN_CHUNKS:
            for i in range(n):
                chunk_tab.append((off, e, "ffn", name, i))
                off += 128 * e
        assert off == 128 * LAYER_ELEMS
        n_mix = sum(n for _, n, _ in MIX_CHUNKS)
        n_ffn = sum(n for _, n, _ in FFN_CHUNKS)
        cast_queue = []
        for l in range(DEPTH):
            for ci, (o, e, ph, name, i) in enumerate(chunk_tab):
                grp = 2 * l + (0 if ph == "mix" else 1)
                gsz = n_mix if ph == "mix" else n_ffn
                if l == 0 and ph == "mix":
                    NFIRST = 2
                    if ci < NFIRST:
                        grp, gsz = 4, NFIRST
                    elif ci < 2 * NFIRST:
                        grp, gsz = 5, NFIRST
                    else:
                        gsz = n_mix - 2 * NFIRST
                cast_queue.append((l, ci, o, e, grp, gsz))

        def cast_some(n):
            for _ in range(n):
                if not cast_queue:
                    return
                l, ci, o, e, grp, gsz = cast_queue.pop(0)
                src = wf[l, o:o + 128 * e].rearrange("(p x) -> p x", p=128)
                dst = wbf[l, o:o + 128 * e].rearrange("(p x) -> p x", p=128)
                late = (l == 0 and ci >= 2 and ci < n_mix)
                S.add("pool", lambda eng, dst=dst, src=src: eng.dma_start(out=dst, in_=src),
                      reads=([("H", 0, 0)] if late else []), writes=[("WBF", l, ci)], dma=(s_cast[grp], 16 * gsz))

        wstate = {"n": 0, "slot_loads": [0] * NSLOT}

        def wload(l, ci):
            o, e, ph, name, i = chunk_tab[ci]
            slot = wstate["n"] % NSLOT
            wstate["n"] += 1
            wstate["slot_loads"][slot] += 1
            src = wbf[l, o:o + 128 * e].rearrange("(p x) -> p x", p=128)
            dst = WR[:, slot, 0:e]
            S.add("sp", lambda eng, dst=dst, src=src: eng.dma_start(out=dst, in_=src),
                  reads=[("WBF", l, ci)], writes=[("WR", slot)],
                  dma=(s_w[slot], 16 * wstate["slot_loads"][slot]))
            return slot

        bank_ctr = [0]

        def newbank():
            b = bank_ctr[0] % 8
            bank_ctr[0] += 1
            return b

        def vcol(l, base, idx):
            c = l * V_LAYER + base + idx
            return VEC[:, c:c + 1]

        def load_x_tile(s, i):
            cols = slice(i * N, (i + 1) * N)
            src = xin[s, i, :, :].rearrange("p (c t) -> p c t", c=8)
            cnt = s + 1
            S.add("sp" if s == 0 else "act", lambda e, src=src, cols=cols: e.dma_start(out=H[:, :, cols], in_=src),
                  writes=[("H", i, c) for c in range(8)], dma=(s_x[i], 16 * cnt))

        def prep(i, gbase_l, gbase, hist, defer_hn=False):
            cols = slice(i * N, (i + 1) * N)
            if i in unloaded:
                flush_sp(final=True)
            for c in range(8):
                S.add("act", lambda e, c=c: e.activation(out=SQ[:, c, :], in_=H[:, c, cols], func=AF.Square),
                      reads=[("H", i, c)], writes=[("SQ", c)])
            b = newbank()
            for c in range(8):
                S.add("pe", lambda e, c=c, b=b: e.matmul(PS[:, b, 0:N], ONES[:, :], SQ[:, c, :], start=(c == 0), stop=(c == 7)),
                      reads=[("SQ", c), ("ONES",)], writes=[("PS", b)])
            S.add("act", lambda e, b=b: e.activation(out=RSTD[:, :], in_=PS[:, b, 0:N], func=AF.Ln, bias=EPSV[:, 0:1]),
                  reads=[("PS", b), ("EPSV",)], writes=[("RSTD",)])
            S.add("act", lambda e: e.activation(out=RSTD[:, :], in_=RSTD[:, :], func=AF.Exp, scale=-0.5),
                  reads=[("RSTD",)], writes=[("RSTD",)])
            if not defer_hn:
                prep_hn(i, gbase_l, gbase, hist)

        def prep_hn(i, gbase_l, gbase, hist):
            cols = slice(i * N, (i + 1) * N)
            if hist == "zero":
                S.add("pool", lambda e: e.memset(HN[:, :, 0:2], 0.0), writes=[("HN", c) for c in range(8)])
            elif hist == "copy":
                S.add("pool", lambda e: e.tensor_copy(out=HN[:, :, 0:2], in_=HN[:, :, N:N + 2]),
                      reads=[("HN", c) for c in range(8)], writes=[("HN", c) for c in range(8)])
            for c in range(8):
                g = vcol(gbase_l, gbase, c)
                r = c % 2
                S.add("pool", lambda e, c=c, r=r: e.tensor_tensor(out=TH[:, r, :], in0=H[:, c, cols], in1=RSTD[:, :], op=ALU.mult),
                      reads=[("H", i, c), ("RSTD",)], writes=[("TH", r)])
                S.add("act", lambda e, c=c, g=g, r=r: e.activation(out=HN[:, c, 2:N + 2], in_=TH[:, r, :], func=AF.Identity, scale=g),
                      reads=[("TH", r), ("VEC",)], writes=[("HN", c)])

        def build_dg(l):
            for c in range(4):
                cw = VEC[:, l * V_LAYER + V_CW + c * CONV_K: l * V_LAYER + V_CW + (c + 1) * CONV_K]
                in0 = IDENT[:, :].unsqueeze(1).broadcast_to([128, CONV_K, 128])
                in1 = cw.unsqueeze(2).broadcast_to([128, CONV_K, 128])
                S.add("dve", lambda e, c=c, in0=in0, in1=in1: e.scalar_tensor_tensor(
                    out=DG[:, c, :, :], in0=in0, scalar=0.5, in1=in1, op0=ALU.mult, op1=ALU.mult),
                    reads=[("IDENT",), ("VEC",)], writes=[("DG", c)], arena=True)

        def front_A_q(s, l, i, q, first=False):
            ub = i % 2
            if first and q == 0:
                S.add("pool", lambda e: e.memset(U[:, 0, :, 0:30], 0.0), writes=[("U", 0, c) for c in range(4)])
                S.add("pool", lambda e: e.memset(P[:, :, 0:16], 0.0), writes=[("P", g) for g in range(4)])
            slot = wload(l, q)
            if q == 3:
                flush_sp()
            for half in range(2):
                b = newbank()
                for k in range(8):
                    S.add("pe", lambda e, b=b, k=k, slot=slot, half=half: e.matmul(
                        PS[:, b, 0:N], WR[:, slot, k * 256 + half * 128: k * 256 + half * 128 + 128], HN[:, k, 2:N + 2],
                        start=(k == 0), stop=(k == 7)),
                        reads=[("WR", slot), ("HN", k)], writes=[("PS", b)])
                if q < 4:
                    c = q
                    if half == 0:
                        S.add("act", lambda e, b=b, c=c: e.activation(out=TH[:, c % 2, :], in_=PS[:, b, 0:N], func=AF.Tanh, scale=0.5),
                              reads=[("PS", b)], writes=[("TH", c % 2)])
                    else:
                        S.add("dve", lambda e, b=b, c=c: e.scalar_tensor_tensor(
                            out=U[:, ub, c, 30:30 + N], in0=TH[:, c % 2, :], scalar=1.0, in1=PS[:, b, 0:N],
                            op0=ALU.add, op1=ALU.mult),
                            reads=[("PS", b), ("TH", c % 2)], writes=[("U", ub, c)])
                else:
                    g = (q - 4) * 2 + half
                    S.add("act", lambda e, b=b, g=g: e.activation(out=P[:, g, 16:16 + N], in_=PS[:, b, 0:N], func=AF.Identity),
                          reads=[("PS", b)], writes=[("P", g)])
            if q == 3:
                S.add("pool", lambda e: e.tensor_copy(out=U[:, 1 - ub, :, 0:30], in_=U[:, ub, :, N:N + 30]),
                      reads=[("U", ub, c) for c in range(4)], writes=[("U", 1 - ub, c) for c in range(4)])

        def front_A(s, l, i, first):
            for q in range(6):
                front_A_q(s, l, i, q, first)

        def conv_ln(s, l, i, hook=None, defer_tail=False):
            ub = i % 2

            cbank = {}

            def conv(c):
                b = newbank()
                cbank[c] = b
                for k in range(CONV_K):
                    S.add("pe", lambda e, b=b, c=c, k=k: e.matmul(PS[:, b, 0:N], DG[:, c, k, :], U[:, ub, c, k:k + N],
                                                                 start=(k == 0), stop=(k == CONV_K - 1)),
                          reads=[("DG", c), ("U", ub, c)], writes=[("PS", b)], arena=True)
                cb = vcol(l, V_CB, c)
                S.add("act", lambda e, b=b, c=c, cb=cb: e.activation(out=CB[:, c, :], in_=PS[:, b, 0:N], func=AF.Identity, bias=cb),
                      reads=[("PS", b), ("VEC",)], writes=[("CB", c)])

            def mean(c):
                b = cbank[c]
                cb = vcol(l, V_CB, c)
                S.add("pe", lambda e, b=b, c=c: e.matmul(PS[:, b, 0:N], NLNB[:, :], CB[:, c, :], start=False, stop=True),
                      reads=[("CB", c), ("NLNB",), ("PS", b)], writes=[("PS", b)])
                S.add("act", lambda e, b=b, c=c, cb=cb: e.activation(out=CF[:, c, :], in_=PS[:, b, 0:N], func=AF.Identity, bias=cb),
                      reads=[("PS", b), ("VEC",)], writes=[("CF", c)])
                S.add("act", lambda e, b=b, c=c, cb=cb: e.activation(out=C2[:, c, :], in_=PS[:, b, 0:N], func=AF.Square, bias=cb),
                      reads=[("PS", b), ("VEC",)], writes=[("C2", c)])

            def var(c):
                b = newbank()
                S.add("pe", lambda e, b=b, c=c: e.matmul(PS[:, b, 0:N], LNB[:, :], C2[:, c, :], start=True, stop=True),
                      reads=[("C2", c), ("LNB",)], writes=[("PS", b)])
                S.add("act", lambda e, b=b, c=c: e.activation(out=RS[:, c % 2, :], in_=PS[:, b, 0:N], func=AF.Ln, bias=EPSV[:, 0:1]),
                      reads=[("PS", b), ("EPSV",)], writes=[("RS", c % 2)])
                S.add("act", lambda e, c=c: e.activation(out=RS[:, c % 2, :], in_=RS[:, c % 2, :], func=AF.Exp, scale=-0.5),
                      reads=[("RS", c % 2)], writes=[("RS", c % 2)])
                S.add("dve", lambda e, c=c: e.tensor_tensor(out=CF[:, c, :], in0=CF[:, c, :], in1=RS[:, c % 2, :], op=ALU.mult),
                      reads=[("CF", c), ("RS", c % 2)], writes=[("CF", c)])

            conv(0)
            conv(1)
            mean(0)
            if hook is not None:
                hook()
            conv(2)
            mean(1)
            var(0)
            conv(3)
            mean(2)
            pool_d(s, l, i)
            var(1)

            def silus():
                for c in range(4):
                    lg, lb = vcol(l, V_LNG, c), vcol(l, V_LNB, c)
                    S.add("act", lambda e, c=c, lg=lg, lb=lb: e.activation(out=Y[:, c, :], in_=CF[:, c, :], func=AF.Silu, bias=lb, scale=lg),
                          reads=[("CF", c), ("VEC",)], writes=[("Y", c)])

            steps = [lambda: mean(3), lambda: (var(2), var(3)), silus]
            if defer_tail:
                return steps
            for f in steps:
                f()
            return []

        pool_cur = {}

        def pool_ops(s, l, i):
            W = N + 16
            A, B = PA4, PB4
            S.add("pool", lambda e: e.tensor_tensor(out=A[:, :, 1:W], in0=P[:, :, 1:W], in1=P[:, :, 0:W - 1], op=ALU.add),
                  reads=[("P", g) for g in range(4)], writes=[("PA", g) for g in range(4)])
            S.add("pool", lambda e: e.tensor_tensor(out=B[:, 0:3, 3:W], in0=A[:, 1:4, 3:W], in1=A[:, 1:4, 1:W - 2], op=ALU.add),
                  reads=[("PA", g) for g in (1, 2, 3)], writes=[("PB", g) for g in (1, 2, 3)])
            S.add("pool", lambda e: e.tensor_tensor(out=A[:, 2:4, 7:W], in0=B[:, 1:3, 7:W], in1=B[:, 1:3, 3:W - 4], op=ALU.add),
                  reads=[("PB", g) for g in (2, 3)], writes=[("PA", g) for g in (2, 3)])
            S.add("pool", lambda e: e.tensor_tensor(out=B[:, 2, 15:W], in0=A[:, 3, 15:W], in1=A[:, 3, 7:W - 8], op=ALU.add),
                  reads=[("PA", 3)], writes=[("PB", 3)])
            if i == 0:
                j0 = 16 + HALO
                for g in range(4):
                    cur, ck, gi = (A, "PA", g) if g in (0, 2) else (B, "PB", g - 1)
                    S.add("pool", lambda e, cur=cur, g=g, gi=gi: e.tensor_tensor(out=PT[:, g, :], in0=cur[:, gi, j0:j0 + 16], in1=PCNT[:, s, g, :], op=ALU.mult),
                          reads=[(ck, g), ("PCNT",)], writes=[("PT", g)])

        def pool_d(s, l, i):
            W = N + 16
            j0 = 16 + HALO
            for g in range(4):
                cur, ck, gi = (PA4, "PA", g) if g in (0, 2) else (PB4, "PB", g - 1)
                inv = 1.0 / float(2 ** (g + 1))
                S.add("dve", lambda e, cur=cur, inv=inv, g=g, gi=gi: e.scalar_tensor_tensor(
                    out=D[:, g, :], in0=cur[:, gi, 16:W], scalar=inv, in1=P[:, g, 16:W], op0=ALU.mult, op1=ALU.subtract),
                    reads=[(ck, g), ("P", g)], writes=[("D", g)])
                if i == 0:
                    S.add("pool", lambda e, g=g: e.tensor_tensor(out=D[:, g, HALO:HALO + 16], in0=PT[:, g, :], in1=P[:, g, j0:j0 + 16], op=ALU.subtract),
                          reads=[("PT", g), ("P", g), ("D", g)], writes=[("D", g)])
            S.add("pool", lambda e: e.tensor_copy(out=P[:, :, 0:16], in_=P[:, :, N:N + 16]),
                  reads=[("P", g) for g in range(4)], writes=[("P", g) for g in range(4)])

        def pool_mm(s, l, i):
            slot = wload(l, 6)
            for g in range(4):
                b = newbank()
                S.add("pe", lambda e, b=b, g=g, slot=slot: e.matmul(PS[:, b, 0:N], WR[:, slot, g * 128:(g + 1) * 128], D[:, g, :], start=True, stop=True),
                      reads=[("WR", slot), ("D", g)], writes=[("PS", b)])
                sc = vcol(l, V_PSC, g)
                S.add("act", lambda e, b=b, g=g, sc=sc: e.activation(out=Y[:, 4 + g, :], in_=PS[:, b, 0:N], func=AF.Identity, scale=sc),
                      reads=[("PS", b), ("VEC",)], writes=[("Y", 4 + g)])

        def back_out(s, l, i):
            cols = slice(i * N, (i + 1) * N)
            for q in range(4):
                slot = wload(l, 7 + q)
                for half in range(2):
                    m = q * 2 + half
                    b = newbank()
                    korder = [0, 1, 2, 3, 4, 5, 6, 7]
                    for kk, k in enumerate(korder):
                        S.add("pe", lambda e, b=b, k=k, kk=kk, slot=slot, half=half: e.matmul(
                            PS[:, b, 0:N], WR[:, slot, k * 256 + half * 128: k * 256 + half * 128 + 128], Y[:, k, :],
                            start=(kk == 0), stop=(kk == 7)),
                            reads=[("WR", slot), ("Y", k)], writes=[("PS", b)])
                    S.add("dve", lambda e, b=b, m=m: e.tensor_tensor(out=H[:, m, cols], in0=H[:, m, cols], in1=PS[:, b, 0:N], op=ALU.add),
                          reads=[("PS", b), ("H", i, m)], writes=[("H", i, m)])
                    if i == 0:
                        S.add("dve", lambda e, m=m: e.tensor_tensor(out=H[:, m, 0:HALO], in0=H[:, m, 0:HALO], in1=MASK[:, s, :], op=ALU.mult),
                              reads=[("H", i, m), ("MASK",)], writes=[("H", i, m)])

        def ffn_bufs(j):
            sl = (j % 2) * 4
            return (FT[:, sl + 0, :], FT[:, sl + 1, :], FT[:, sl + 2, :], FT[:, sl + 3, :],
                    ("FT", sl), ("FT", sl + 1), ("FT", sl + 2), ("FT", sl + 3))

        def ffn_pair_head(l, j, sk=0):
            W = N - sk
            slot = wload(l, n_mix + j)
            bg, bv = newbank(), newbank()
            for half, b in ((0, bg), (1, bv)):
                for k in range(8):
                    S.add("pe", lambda e, b=b, k=k, slot=slot, half=half: e.matmul(
                        PS[:, b, 0:W + 2], WR[:, slot, k * 256 + half * 128: k * 256 + half * 128 + 128], HN[:, k, sk:N + 2],
                        start=(k == 0), stop=(k == 7)),
                        reads=[("WR", slot), ("HN", k)], writes=[("PS", b)])
            GA, GB, VA, VB, kA, kB, kVA, kVB = ffn_bufs(j)
            GA, GB, VA, VB = GA[:, 0:W], GB[:, 0:W], VA[:, 0:W], VB[:, 0:W]

            def fk(tap, ch):
                cidx = l * V_LAYER + V_FK + tap * 2 * NPAIR + ch
                return VEC[:, cidx:cidx + 1]
            S.add("act", lambda e: e.activation(out=GA, in_=PS[:, bg, 0:W], func=AF.Identity, scale=fk(0, j)),
                  reads=[("PS", bg), ("VEC",)], writes=[kA], arena=True)
            S.add("act", lambda e: e.activation(out=VA, in_=PS[:, bv, 0:W], func=AF.Identity, scale=fk(0, NPAIR + j)),
                  reads=[("PS", bv), ("VEC",)], writes=[kVA], arena=True)
            S.add("act", lambda e: e.activation(out=GB, in_=PS[:, bg, 1:W + 1], func=AF.Identity, scale=fk(1, j)),
                  reads=[("PS", bg), ("VEC",)], writes=[kB], arena=True)
            S.add("pool", lambda e: e.tensor_tensor(out=GB, in0=GB, in1=GA, op=ALU.add),
                  reads=[kA, kB], writes=[kB], arena=True)
            S.add("dve", lambda e: e.scalar_tensor_tensor(out=GA, in0=PS[:, bg, 2:W + 2], scalar=fk(2, j), in1=GB, op0=ALU.mult, op1=ALU.add),
                  reads=[("PS", bg), kB, ("VEC",)], writes=[kA], arena=True)
            S.add("dve", lambda e: e.scalar_tensor_tensor(out=VB, in0=PS[:, bv, 1:W + 1], scalar=fk(1, NPAIR + j), in1=VA, op0=ALU.mult, op1=ALU.add),
                  reads=[("PS", bv), kVA, ("VEC",)], writes=[kVB], arena=True)
            S.add("dve", lambda e: e.scalar_tensor_tensor(out=VA, in0=PS[:, bv, 2:W + 2], scalar=fk(2, NPAIR + j), in1=VB, op0=ALU.mult, op1=ALU.add),
                  reads=[("PS", bv), kVB, ("VEC",)], writes=[kVA], arena=True)

        def ffn_pair_tail(l, j, sk=0):
            W = N - sk
            GA, GB, VA, VB, kA, kB, kVA, kVB = ffn_bufs(j)
            GA, GB, VA, VB = GA[:, 0:W], GB[:, 0:W], VA[:, 0:W], VB[:, 0:W]
            S.add("act", lambda e: e.activation(out=GB, in_=GA, func=AF.Silu),
                  reads=[kA], writes=[kB], arena=True)
            S.add("pool", lambda e: e.tensor_tensor(out=ACTV[:, j, 0:W], in0=VA, in1=GB, op=ALU.mult),
                  reads=[kVA, kB], writes=[("ACTV", j)], arena=True)

        def ffn_down(l, i, m, sk=0):
            W = N - sk
            cols = slice(i * N + sk, (i + 1) * N)
            slot = wload(l, n_mix + NPAIR + m)
            b = newbank()
            for j in range(NPAIR):
                S.add("pe", lambda e, b=b, j=j, slot=slot: e.matmul(PS[:, b, 0:W], WR[:, slot, j * 128:(j + 1) * 128], ACTV[:, j, 0:W],
                                                                   start=(j == 0), stop=(j == NPAIR - 1)),
                      reads=[("WR", slot), ("ACTV", j)], writes=[("PS", b)], arena=True)
            S.add("dve", lambda e, b=b, m=m: e.tensor_tensor(out=H[:, m, cols], in0=H[:, m, cols], in1=PS[:, b, 0:W], op=ALU.add),
                  reads=[("PS", b), ("H", i, m)], writes=[("H", i, m)])

        def ffn_down_first2(l, i, sk=0):
            W = N - sk
            cols = slice(i * N + sk, (i + 1) * N)
            slots = [wload(l, n_mix + NPAIR + m) for m in (0, 1)]
            banks = [newbank(), newbank()]
            for m in (0, 1):
                for j in range(NPAIR - 1):
                    S.add("pe", lambda e, b=banks[m], j=j, slot=slots[m]: e.matmul(
                        PS[:, b, 0:W], WR[:, slot, j * 128:(j + 1) * 128], ACTV[:, j, 0:W], start=(j == 0), stop=False),
                        reads=[("WR", slots[m]), ("ACTV", j)], writes=[("PS", banks[m])], arena=True)
            j = NPAIR - 1
            for m in (0, 1):
                S.add("pe", lambda e, b=banks[m], slot=slots[m]: e.matmul(
                    PS[:, b, 0:W], WR[:, slot, j * 128:(j + 1) * 128], ACTV[:, j, 0:W], start=False, stop=True),
                    reads=[("WR", slots[m]), ("ACTV", j)], writes=[("PS", banks[m])], arena=True)
                S.add("dve", lambda e, b=banks[m], m=m: e.tensor_tensor(out=H[:, m, cols], in0=H[:, m, cols], in1=PS[:, b, 0:W], op=ALU.add),
                      reads=[("PS", banks[m]), ("H", i, m)], writes=[("H", i, m)])

        def final_tile(s, i):
            cols = slice(i * N, (i + 1) * N)
            for c in range(8):
                S.add("act", lambda e, c=c: e.activation(out=SQ[:, c, :], in_=H[:, c, cols], func=AF.Square),
                      reads=[("H", i, c)], writes=[("SQ", c)])
            b = newbank()
            for c in range(8):
                S.add("pe", lambda e, c=c, b=b: e.matmul(PS[:, b, 0:N], ONES[:, :], SQ[:, c, :], start=(c == 0), stop=(c == 7)),
                      reads=[("SQ", c), ("ONES",)], writes=[("PS", b)])
            S.add("act", lambda e, b=b: e.activation(out=RSTD[:, :], in_=PS[:, b, 0:N], func=AF.Ln, bias=EPSV[:, 0:1]),
                  reads=[("PS", b), ("EPSV",)], writes=[("RSTD",)])
            S.add("act", lambda e: e.activation(out=RSTD[:, :], in_=RSTD[:, :], func=AF.Exp, scale=-0.5),
                  reads=[("RSTD",)], writes=[("RSTD",)])
            for c in range(8):
                fg = VEC[:, DEPTH * V_LAYER + c: DEPTH * V_LAYER + c + 1]
                S.add("pool", lambda e, c=c: e.tensor_tensor(out=H[:, c, cols], in0=H[:, c, cols], in1=RSTD[:, :], op=ALU.mult),
                      reads=[("H", i, c), ("RSTD",)], writes=[("H", i, c)])
                S.add("act", lambda e, c=c, fg=fg: e.activation(out=H[:, c, cols], in_=H[:, c, cols], func=AF.Identity, scale=fg),
                      reads=[("H", i, c), ("VEC",)], writes=[("H", i, c)])
            lo = i * N
            hi = (i + 1) * N
            dst = out_d[s, i, :, :].rearrange("p (c t) -> p c t", c=8)
            cnt = s + 1

            def do_store(dst=dst, lo=lo, hi=hi, i=i, cnt=cnt):
                S.add("act", lambda e: e.dma_start(out=dst, in_=H[:, :, lo:hi]),
                      reads=[("H", i, c) for c in range(8)], dma=(s_out[i], 16 * cnt))
            pending_sp.append(do_store)

        unloaded = set()
        pending_sp = []
        pending_ld = []

        def flush_sp(final=False):
            lds = list(pending_ld)
            del pending_ld[:]
            for f in lds:
                f()
            while pending_sp:
                pending_sp.pop(0)()
            if final:
                lds = list(pending_ld)
                del pending_ld[:]
                for f in lds:
                    f()

        load_x_tile(0, 0)
        cast_some(n_mix)
        prep(0, 0, V_N1G, None)
        first_loads = [True]
        NPRE = 0
        for s in range(NSEG):
            for l in range(DEPTH):
                last = (l == DEPTH - 1)
                S.arena_handoff()
                front_A(s, l, 0, first=True)
                if first_loads[0]:
                    first_loads[0] = False
                    for i in range(1, NT):
                        load_x_tile(0, i)
                build_dg(l)
                for i in range(NT):
                    if i + 1 < NT:
                        prep(i + 1, l, V_N1G, None, defer_hn=True)
                        hook = (lambda i=i, l=l: prep_hn(i + 1, l, V_N1G, None))
                    else:
                        prep(0, l, V_N2G, "zero", defer_hn=True)
                        hook = (lambda l=l: prep_hn(0, l, V_N2G, "zero"))
                    if i == 0:
                        pool_ops(s, l, 0)
                    steps = conv_ln(s, l, i, hook, defer_tail=(i + 1 < NT))
                    if s == 0 and l == 0:
                        cast_some((n_ffn + NT - 1) // NT)
                    if i + 1 < NT:
                        for q in range(4):
                            front_A_q(s, l, i + 1, q)
                            if q < len(steps):
                                steps[q]()
                        pool_mm(s, l, i)
                        front_A_q(s, l, i + 1, 4)
                        front_A_q(s, l, i + 1, 5)
                    else:
                        S.arena_handoff()
                        pool_mm(s, l, i)
                    back_out(s, l, i)
                    if i + 1 < NT:
                        pool_ops(s, l, i + 1)
                for i in range(NT):
                    sk = max(0, 64 - 32 * (DEPTH - 1 - l)) if i == 0 else 0
                    for j in range(NPRE if i == 0 else 0, NPAIR):
                        ffn_pair_head(l, j, sk)
                        if j >= 1:
                            ffn_pair_tail(l, j - 1, sk)
                        if j == 8:
                            flush_sp()
                    ffn_pair_tail(l, NPAIR - 1, sk)
                    if s == 0 and l == 0:
                        cast_some(((DEPTH - 1) * (n_mix + n_ffn) + NT - 1) // NT)
                    ffn_down_first2(l, i, sk)
                    for m in range(2, 4):
                        ffn_down(l, i, m, sk)
                    if i + 1 < NT:
                        prep(i + 1, l, V_N2G, "copy")
                    elif not last:
                        prep(0, l + 1, V_N1G, None)
                    elif s + 1 < NSEG:
                        flush_sp(final=(NT < 3))
                        prep(0, 0, V_N1G, None)
                    for m in range(4, 8):
                        ffn_down(l, i, m, sk)
                    if last:
                        final_tile(s, i)
                        if s + 1 < NSEG:
                            unloaded.add(i)
                            pending_sp.append(lambda s=s, i=i: pending_ld.append(lambda: (unloaded.discard(i), load_x_tile(s + 1, i))))
        flush_sp(final=True)
        assert not cast_queue
        S.add("sp", lambda e: e.nop(), writes=[("H", i, c) for i in range(NT) for c in range(8)])

        S.emit_all(nc,
                   {"pe": block.tensor, "act": block.scalar, "dve": block.vector, "pool": block.gpsimd, "sp": block.sync},
                   {"pe": s_pe, "act": s_act, "dve": s_dve, "pool": s_pool})
    return nc


def _layout_weights(l, w_in, pool_w, w_out, w_up, w_down):
    parts = []
    W = w_in[l]
    blocks = []
    for c in range(4):
        blocks.append(W[:, 512 + c * 128: 512 + (c + 1) * 128])
        blocks.append(W[:, c * 128:(c + 1) * 128])
    for g in range(4):
        blocks.append(W[:, 1024 + g * 128: 1024 + (g + 1) * 128])
    Wp = np.concatenate(blocks, axis=1)
    parts.append(Wp.reshape(8, 128, 6, 256).transpose(2, 1, 0, 3).reshape(-1))
    parts.append(pool_w[l].transpose(1, 0, 2).reshape(-1))
    parts.append(w_out[l].reshape(8, 128, 4, 256).transpose(2, 1, 0, 3).reshape(-1))
    Wu = w_up[l]
    up = np.concatenate([Wu[:, :D_FF].reshape(8, 128, NPAIR, 1, 128), Wu[:, D_FF:].reshape(8, 128, NPAIR, 1, 128)], axis=3)
    parts.append(up.transpose(2, 1, 0, 3, 4).reshape(-1))
    parts.append(w_down[l].reshape(NPAIR, 128, 8, 128).transpose(2, 1, 0, 3).reshape(-1))
    flat = np.concatenate(parts).astype(np.float32, copy=False)
    assert flat.size == 128 * LAYER_ELEMS
    return flat


def _layout_vecs(DEPTH, norm1_g, conv_dw_k, conv_dw_b, conv_ln_g, conv_ln_b, pool_scale, norm2_g, ffn_dw_k, final_g):
    v = np.zeros((128, DEPTH * V_LAYER + 8), np.float32)
    for l in range(DEPTH):
        o = l * V_LAYER
        v[:, o + V_N1G:o + V_N1G + 8] = norm1_g[l].reshape(8, 128).T
        v[:, o + V_N2G:o + V_N2G + 8] = norm2_g[l].reshape(8, 128).T
        v[:, o + V_CB:o + V_CB + 4] = conv_dw_b[l].reshape(4, 128).T
        v[:, o + V_LNG:o + V_LNG + 4] = conv_ln_g[l].reshape(4, 128).T
        v[:, o + V_LNB:o + V_LNB + 4] = conv_ln_b[l].reshape(4, 128).T
        v[:, o + V_PSC:o + V_PSC + 4] = pool_scale[l].reshape(4, 128).T
        v[:, o + V_CW:o + V_CW + 4 * CONV_K] = conv_dw_k[l].reshape(CONV_K, 4, 128).transpose(2, 1, 0).reshape(128, 4 * CONV_K)
        v[:, o + V_FK:o + V_FK + 3 * 2 * NPAIR] = ffn_dw_k[l].reshape(3, 2 * NPAIR, 128).transpose(2, 0, 1).reshape(128, 3 * 2 * NPAIR)
    v[:, DEPTH * V_LAYER:DEPTH * V_LAYER + 8] = final_g.reshape(8, 128).T
    return v


def _consts():
    c = np.zeros((128, 512), np.float32)
    c[:, 0:128] = np.eye(128, dtype=np.float32)
    c[:, 128:256] = 1.0 / 1024.0
    blk = np.zeros((128, 128), np.float32)
    blk[:64, :64] = 1.0 / 64.0
    blk[64:, 64:] = 1.0 / 64.0
    c[:, 256:384] = blk
    c[:, 384:512] = -blk
    return c


_PROG_CACHE = {}


def run_model(x, meta_tokens, norm1_g, w_in, conv_dw_k, conv_dw_b, conv_ln_g, conv_ln_b, pool_w, pool_scale,
              w_out, norm2_g, w_up, ffn_dw_k, w_down, final_g, NT, NSEG, NCORES, NQ, DEPTH):
    x = np.asarray(x, np.float32)
    B, SEQ, _ = x.shape
    TSEG = NT * N
    OWN = TSEG - HALO
    L = SEQ + N_META
    assert L == NQ * OWN and B * NQ == NCORES * NSEG
    args = [np.asarray(a, np.float32) for a in (norm1_g, w_in, conv_dw_k, conv_dw_b, conv_ln_g, conv_ln_b, pool_w,
                                                 pool_scale, w_out, norm2_g, w_up, ffn_dw_k, w_down, final_g)]
    norm1_g, w_in, conv_dw_k, conv_dw_b, conv_ln_g, conv_ln_b, pool_w, pool_scale, w_out, norm2_g, w_up, ffn_dw_k, w_down, final_g = args
    wf = np.stack([_layout_weights(l, w_in, pool_w, w_out, w_up, w_down) for l in range(DEPTH)], 0)
    vecs = _layout_vecs(DEPTH, norm1_g, conv_dw_k, conv_dw_b, conv_ln_g, conv_ln_b, pool_scale, norm2_g, ffn_dw_k, final_g)
    consts = _consts()
    meta = np.asarray(meta_tokens, np.float32)
    in_maps = []
    for core in range(NCORES):
        xin = np.zeros((NSEG, NT, 128, 8 * N), np.float32)
        mask = np.ones((128, NSEG, HALO), np.float32)
        pcnt = np.zeros((128, NSEG, 4, 16), np.float32)
        for s in range(NSEG):
            q = core * NSEG + s
            b, r = divmod(q, NQ)
            p0 = r * OWN - HALO
            seg = np.zeros((TSEG, D_MODEL), np.float32)
            lo = max(p0, 0)
            hi = p0 + TSEG
            hcat_rows = []
            if lo < N_META:
                hcat_rows.append(meta[lo:min(hi, N_META)])
            if hi > N_META:
                hcat_rows.append(x[b, max(lo, N_META) - N_META: hi - N_META])
            seg[lo - p0:] = np.concatenate(hcat_rows, 0)
            xin[s] = seg.T.reshape(8, 128, NT, N).transpose(2, 1, 0, 3).reshape(NT, 128, 8 * N)
            for g in range(4):
                w = 2 ** (g + 1)
                if r == 0:
                    mask[:, s] = 0.0
                    pcnt[:, s, g, :] = 1.0 / np.minimum(np.arange(16) + 1, w).astype(np.float32)
                else:
                    pcnt[:, s, g, :] = 1.0 / w
        in_maps.append({"xin": xin, "wf": wf, "vecs": vecs, "consts": consts,
                        "mask": mask.reshape(128, -1), "pcnt": pcnt.reshape(128, -1)})
    key = (NT, NSEG, DEPTH)
    if key not in _PROG_CACHE:
        _PROG_CACHE[key] = build_program(NT, NSEG, DEPTH)
    nc = _PROG_CACHE[key]
    res = run_bass_kernel_spmd(nc, in_maps, core_ids=list(range(NCORES)))
    y = np.empty((B, SEQ, D_MODEL), np.float32)
    for core in range(NCORES):
        o = res.results[core]["out"]
        for s in range(NSEG):
            q = core * NSEG + s
            b, r = divmod(q, NQ)
            full = o[s].reshape(NT, 128, 8, N).transpose(2, 1, 0, 3).reshape(D_MODEL, TSEG).T
            tok = full[HALO:]
            p_lo = r * OWN
            if r == 0:
                y[b, 0:OWN - N_META] = tok[N_META:]
            else:
                y[b, p_lo - N_META:p_lo - N_META + OWN] = tok
    return y


def kernel(x, meta_tokens, norm1_g, w_in, conv_dw_k, conv_dw_b, conv_ln_g, conv_ln_b, pool_w, pool_scale,
           w_out, norm2_g, w_up, ffn_dw_k, w_down, final_g):
    return run_model(x, meta_tokens, norm1_g, w_in, conv_dw_k, conv_dw_b, conv_ln_g, conv_ln_b, pool_w, pool_scale,
                     w_out, norm2_g, w_up, ffn_dw_k, w_down, final_g, NT=5, NSEG=2, NCORES=8, NQ=4, DEPTH=2)
```

```python
import numpy as np
from contextlib import ExitStack
import concourse.bass as bass
import concourse.mybir as mybir
from concourse.bass_utils import run_bass_kernel_spmd

F32 = mybir.dt.float32
BF16 = mybir.dt.bfloat16
AF = mybir.ActivationFunctionType
ALU = mybir.AluOpType

D_MODEL = 1024
N_META = 16
D_FF = 2816
NPAIR = D_FF // 128
CONV_K = 31
EPS = 1e-6
N = 424
HALO = 68
NSLOT = 6
SLOT_ELEMS = 2816

MIX_CHUNKS = [("win", 6, 2048), ("pw", 1, 512), ("wout", 4, 2048)]
FFN_CHUNKS = [("wup", NPAIR, 2048), ("wdn", 8, 2816)]
LAYER_ELEMS = sum(n * e for _, n, e in MIX_CHUNKS + FFN_CHUNKS)

V_N1G, V_N2G, V_CB, V_LNG, V_LNB, V_PSC, V_CW, V_FK = 0, 8, 16, 20, 24, 28, 32, 156
V_LAYER = 288


class Op:
    __slots__ = ("eng", "emit", "deps", "signal", "sigcount", "is_dma", "dsem", "dval", "idx")


class Sched:
    COMPUTE = ("pe", "act", "dve", "pool")

    def __init__(self):
        self.ops = []
        self.last_writer = {}
        self.readers = {}
        self.last_on = {}
        self.arena_prev = []
        self.arena_cur = {}

    def add(self, eng, emit, reads=(), writes=(), dma=None, arena=False):
        op = Op()
        op.eng = eng
        op.emit = emit
        op.signal = False
        op.sigcount = 0
        op.is_dma = dma is not None
        op.dsem, op.dval = dma if dma is not None else (None, None)
        op.idx = len(self.ops)
        deps = {}

        def need(d, kind):
            if d is op:
                return
            if (not d.is_dma) and (not op.is_dma) and d.eng == op.eng:
                if d.eng == "pe":
                    return
                if kind != "RAW":
                    return
            deps[d.idx] = d

        for k in reads:
            w = self.last_writer.get(k)
            if w is not None:
                need(w, "RAW")
        for k in writes:
            w = self.last_writer.get(k)
            if w is not None:
                need(w, "WAW")
            for r in self.readers.get(k, {}).values():
                need(r, "WAR")
        if arena:
            for d in self.arena_prev:
                need(d, "WAR")
            self.arena_cur[("dma", op.idx) if op.is_dma else op.eng] = op
        for k in writes:
            self.last_writer[k] = op
            self.readers[k] = {}
        for k in reads:
            rk = ("dma", op.idx) if op.is_dma else op.eng
            self.readers.setdefault(k, {})[rk] = op
        op.deps = list(deps.values())
        for d in op.deps:
            if not d.is_dma:
                d.signal = True
        self.ops.append(op)
        self.last_on[op.eng] = op
        return op

    def arena_handoff(self):
        self.arena_prev = list(self.arena_cur.values())
        self.arena_cur = {}

    def emit_all(self, nc, block_fns, sems):
        cnt = {e: 0 for e in self.COMPUTE}
        for op in self.ops:
            if not op.is_dma and op.signal:
                cnt[op.eng] += 1
                op.sigcount = cnt[op.eng]
        per_eng = {}
        for op in self.ops:
            per_eng.setdefault(op.eng, []).append(op)

        def token(d):
            if d.is_dma:
                return d.dsem, d.dval
            return sems[d.eng], d.sigcount

        def make(engname, oplist):
            def body(e):
                waited = {}
                for op in oplist:
                    for d in op.deps:
                        s, v = token(d)
                        key = id(s)
                        if waited.get(key, 0) >= v:
                            continue
                        e.wait_ge(s, v)
                        waited[key] = v
                    ins = op.emit(e)
                    if op.is_dma:
                        ins.then_inc(op.dsem, 16)
                    elif op.signal:
                        ins.then_inc(sems[op.eng], 1)
            return body

        for engname, oplist in per_eng.items():
            block_fns[engname](make(engname, oplist))


def build_program(NT, NSEG, DEPTH, debug_h=False):
    TSEG = NT * N
    OWN = TSEG - HALO
    nc = bass.Bass("TRN2", target_bir_lowering=False)
    xin = nc.dram_tensor("xin", [NSEG, NT, 128, 8 * N], F32, kind="ExternalInput").ap()
    wf = nc.dram_tensor("wf", [DEPTH, 128 * LAYER_ELEMS], F32, kind="ExternalInput").ap()
    vecs_d = nc.dram_tensor("vecs", [128, DEPTH * V_LAYER + 8], F32, kind="ExternalInput").ap()
    consts_d = nc.dram_tensor("consts", [128, 4 * 128], F32, kind="ExternalInput").ap()
    mask_d = nc.dram_tensor("mask", [128, NSEG * HALO], F32, kind="ExternalInput").ap()
    pcnt_d = nc.dram_tensor("pcnt", [128, NSEG * 4 * 16], F32, kind="ExternalInput").ap()
    out_d = nc.dram_tensor("out", [NSEG, NT, 128, 8 * N], F32, kind="ExternalOutput").ap()
    wbf = nc.dram_tensor("wbf", [DEPTH, 128 * LAYER_ELEMS], BF16).ap()

    S = Sched()
    ARENA_BYTES = max(4 * CONV_K * 128 * 2, NPAIR * N * 2 + 8 * N * 4)
    with ExitStack() as es:
        H = es.enter_context(nc.sbuf_tensor("H", [128, 8, TSEG], F32))
        ARENA = es.enter_context(nc.sbuf_tensor("ARENA", [128, ARENA_BYTES // 2], BF16))
        U = es.enter_context(nc.sbuf_tensor("U", [128, 2, 4, N + 30], BF16))
        HN = es.enter_context(nc.sbuf_tensor("HN", [128, 8, N + 2], BF16))
        RSTD = es.enter_context(nc.sbuf_tensor("RSTD", [128, N], F32))
        SQ = es.enter_context(nc.sbuf_tensor("SQ", [128, 8, N], BF16))
        TH = es.enter_context(nc.sbuf_tensor("TH", [128, 2, N], F32))
        CF = es.enter_context(nc.sbuf_tensor("CF", [128, 4, N], F32))
        CB = es.enter_context(nc.sbuf_tensor("CB", [128, 4, N], BF16))
        C2 = es.enter_context(nc.sbuf_tensor("C2", [128, 4, N], BF16))
        RS = es.enter_context(nc.sbuf_tensor("RS", [128, 2, N], F32))
        Y = es.enter_context(nc.sbuf_tensor("Y", [128, 8, N], BF16))
        P = es.enter_context(nc.sbuf_tensor("P", [128, 4, N + 16], F32))
        PA4 = es.enter_context(nc.sbuf_tensor("PA4", [128, 4, N + 16], F32))
        PB4 = es.enter_context(nc.sbuf_tensor("PB4", [128, 3, N + 16], F32))
        PT = es.enter_context(nc.sbuf_tensor("PT", [128, 4, 16], F32))
        D = es.enter_context(nc.sbuf_tensor("D", [128, 4, N], BF16))
        WR = es.enter_context(nc.sbuf_tensor("WR", [128, NSLOT, SLOT_ELEMS], BF16))
        IDENT = es.enter_context(nc.sbuf_tensor("IDENT", [128, 128], F32))
        ONES = es.enter_context(nc.sbuf_tensor("ONES", [128, 128], BF16))
        LNB = es.enter_context(nc.sbuf_tensor("LNB", [128, 128], BF16))
        NLNB = es.enter_context(nc.sbuf_tensor("NLNB", [128, 128], BF16))
        VEC = es.enter_context(nc.sbuf_tensor("VEC", [128, DEPTH * V_LAYER + 8], F32))
        MASK = es.enter_context(nc.sbuf_tensor("MASK", [128, NSEG, HALO], F32))
        PCNT = es.enter_context(nc.sbuf_tensor("PCNT", [128, NSEG, 4, 16], F32))
        EPSV = es.enter_context(nc.sbuf_tensor("EPSV", [128, 1], F32))
        PS = es.enter_context(nc.psum_tensor("PS", [128, 8, 512], F32))
        s_pe = es.enter_context(nc.semaphore("s_pe"))
        s_act = es.enter_context(nc.semaphore("s_act"))
        s_dve = es.enter_context(nc.semaphore("s_dve"))
        s_pool = es.enter_context(nc.semaphore("s_pool"))
        s_cst = es.enter_context(nc.semaphore("s_cst"))
        s_cstb = es.enter_context(nc.semaphore("s_cstb"))
        s_o0 = es.enter_context(nc.semaphore("s_o0"))
        s_o1 = es.enter_context(nc.semaphore("s_o1"))
        s_o2 = es.enter_context(nc.semaphore("s_o2"))
        s_o3 = es.enter_context(nc.semaphore("s_o3"))
        s_o4 = es.enter_context(nc.semaphore("s_o4"))
        s_castA = es.enter_context(nc.semaphore("s_castA"))
        s_castB = es.enter_context(nc.semaphore("s_castB"))
        s_castC = es.enter_context(nc.semaphore("s_castC"))
        s_castD = es.enter_context(nc.semaphore("s_castD"))
        s_castE = es.enter_context(nc.semaphore("s_castE"))
        s_castF = es.enter_context(nc.semaphore("s_castF"))
        s_w0 = es.enter_context(nc.semaphore("s_w0"))
        s_w1 = es.enter_context(nc.semaphore("s_w1"))
        s_w2 = es.enter_context(nc.semaphore("s_w2"))
        s_w3 = es.enter_context(nc.semaphore("s_w3"))
        s_w4 = es.enter_context(nc.semaphore("s_w4"))
        s_w5 = es.enter_context(nc.semaphore("s_w5"))
        s_x0 = es.enter_context(nc.semaphore("s_x0"))
        s_x1 = es.enter_context(nc.semaphore("s_x1"))
        s_x2 = es.enter_context(nc.semaphore("s_x2"))
        s_x3 = es.enter_context(nc.semaphore("s_x3"))
        s_x4 = es.enter_context(nc.semaphore("s_x4"))
        block = es.enter_context(nc.Block())
        s_w = [s_w0, s_w1, s_w2, s_w3, s_w4, s_w5]
        s_x = [s_x0, s_x1, s_x2, s_x3, s_x4]
        s_out = [s_o0, s_o1, s_o2, s_o3, s_o4]
        s_cast = [s_castA, s_castB, s_castC, s_castD, s_castE, s_castF]
        DG = ARENA[:, 0:4 * CONV_K * 128].rearrange("p (c k m) -> p c k m", c=4, k=CONV_K)
        ACTV = ARENA[:, 0:NPAIR * N].rearrange("p (j n) -> p j n", j=NPAIR)
        FT = ARENA[:, NPAIR * N:NPAIR * N + 8 * N * 2].bitcast(F32).rearrange("p (a n) -> p a n", a=8)

        n_cst = [0]

        def cst_load(dst, src, keys):
            n_cst[0] += 1
            S.add("sp", lambda e, dst=dst, src=src: e.dma_start(out=dst, in_=src), writes=keys, dma=(s_cst, None))

        S.add("dve", lambda e: e.memset(EPSV[:, :], EPS), writes=[("EPSV",)])
        cst_ops_start = len(S.ops)
        cst_load(VEC[:, :], vecs_d[:, :], [("VEC",)])
        cst_load(IDENT[:, :], consts_d[:, 0:128], [("IDENT",)])
        cst_load(MASK[:, :, :].rearrange("p s h -> p (s h)"), mask_d[:, :], [("MASK",)])
        cst_load(PCNT[:, :, :, :].rearrange("p s g t -> p (s g t)"), pcnt_d[:, :], [("PCNT",)])
        for op in S.ops[cst_ops_start:]:
            op.dval = 16 * n_cst[0]
        o1 = S.add("pool", lambda e: e.dma_start(out=ONES[:, :], in_=consts_d[:, 128:256]), writes=[("ONES",)], dma=(s_cstb, 32))
        o2 = S.add("pool", lambda e: e.dma_start(out=LNB[:, :], in_=consts_d[:, 256:384]), writes=[("LNB",)], dma=(s_cstb, 48))
        o3 = S.add("pool", lambda e: e.dma_start(out=NLNB[:, :], in_=consts_d[:, 384:512]), writes=[("NLNB",)], dma=(s_cstb, 48))
        o1.dval = 48

        chunk_tab = []
        off = 0
        for name, n, e in MIX_CHUNKS:
            for i in range(n):
                chunk_tab.append((off, e, "mix", name, i))
                off += 128 * e
        for name, n, e in FFN_CHUNKS:
            for i in range(n):
                chunk_tab.append((off, e, "ffn", name, i))
                off += 128 * e
        assert off == 128 * LAYER_ELEMS
        n_mix = sum(n for _, n, _ in MIX_CHUNKS)
        n_ffn = sum(n for _, n, _ in FFN_CHUNKS)
        cast_queue = []
        for l in range(DEPTH):
            for ci, (o, e, ph, name, i) in enumerate(chunk_tab):
                grp = 2 * l + (0 if ph == "mix" else 1)
                gsz = n_mix if ph == "mix" else n_ffn
                if l == 0 and ph == "mix":
                    NFIRST = 2
                    if ci < NFIRST:
                        grp, gsz = 4, NFIRST
                    elif ci < 2 * NFIRST:
                        grp, gsz = 5, NFIRST
                    else:
                        gsz = n_mix - 2 * NFIRST
                cast_queue.append((l, ci, o, e, grp, gsz))

        def cast_some(n):
            for _ in range(n):
                if not cast_queue:
                    return
                l, ci, o, e, grp, gsz = cast_queue.pop(0)
                src = wf[l, o:o + 128 * e].rearrange("(p x) -> p x", p=128)
                dst = wbf[l, o:o + 128 * e].rearrange("(p x) -> p x", p=128)
                late = (l == 0 and ci >= 2 and ci < n_mix)
                S.add("pool", lambda eng, dst=dst, src=src: eng.dma_start(out=dst, in_=src),
                      reads=([("H", 0, 0)] if late else []), writes=[("WBF", l, ci)], dma=(s_cast[grp], 16 * gsz))

        wstate = {"n": 0, "slot_loads": [0] * NSLOT}

        def wload(l, ci):
            o, e, ph, name, i = chunk_tab[ci]
            slot = wstate["n"] % NSLOT
            wstate["n"] += 1
            wstate["slot_loads"][slot] += 1
            src = wbf[l, o:o + 128 * e].rearrange("(p x) -> p x", p=128)
            dst = WR[:, slot, 0:e]
            S.add("sp", lambda eng, dst=dst, src=src: eng.dma_start(out=dst, in_=src),
                  reads=[("WBF", l, ci)], writes=[("WR", slot)],
                  dma=(s_w[slot], 16 * wstate["slot_loads"][slot]))
            return slot

        bank_ctr = [0]

        def newbank():
            b = bank_ctr[0] % 8
            bank_ctr[0] += 1
            return b

        def vcol(l, base, idx):
            c = l * V_LAYER + base + idx
            return VEC[:, c:c + 1]

        def load_x_tile(s, i):
            cols = slice(i * N, (i + 1) * N)
            src = xin[s, i, :, :].rearrange("p (c t) -> p c t", c=8)
            cnt = s + 1
            S.add("sp" if s == 0 else "act", lambda e, src=src, cols=cols: e.dma_start(out=H[:, :, cols], in_=src),
                  writes=[("H", i, c) for c in range(8)], dma=(s_x[i], 16 * cnt))

        def prep(i, gbase_l, gbase, hist, defer_hn=False):
            cols = slice(i * N, (i + 1) * N)
            if i in unloaded:
                flush_sp(final=True)
            for c in range(8):
                S.add("act", lambda e, c=c: e.activation(out=SQ[:, c, :], in_=H[:, c, cols], func=AF.Square),
                      reads=[("H", i, c)], writes=[("SQ", c)])
            b = newbank()
            for c in range(8):
                S.add("pe", lambda e, c=c, b=b: e.matmul(PS[:, b, 0:N], ONES[:, :], SQ[:, c, :], start=(c == 0), stop=(c == 7)),
                      reads=[("SQ", c), ("ONES",)], writes=[("PS", b)])
            S.add("act", lambda e, b=b: e.activation(out=RSTD[:, :], in_=PS[:, b, 0:N], func=AF.Ln, bias=EPSV[:, 0:1]),
                  reads=[("PS", b), ("EPSV",)], writes=[("RSTD",)])
            S.add("act", lambda e: e.activation(out=RSTD[:, :], in_=RSTD[:, :], func=AF.Exp, scale=-0.5),
                  reads=[("RSTD",)], writes=[("RSTD",)])
            if not defer_hn:
                prep_hn(i, gbase_l, gbase, hist)

        def prep_hn(i, gbase_l, gbase, hist):
            cols = slice(i * N, (i + 1) * N)
            if hist == "zero":
                S.add("pool", lambda e: e.memset(HN[:, :, 0:2], 0.0), writes=[("HN", c) for c in range(8)])
            elif hist == "copy":
                S.add("pool", lambda e: e.tensor_copy(out=HN[:, :, 0:2], in_=HN[:, :, N:N + 2]),
                      reads=[("HN", c) for c in range(8)], writes=[("HN", c) for c in range(8)])
            for c in range(8):
                g = vcol(gbase_l, gbase, c)
                r = c % 2
                S.add("pool", lambda e, c=c, r=r: e.tensor_tensor(out=TH[:, r, :], in0=H[:, c, cols], in1=RSTD[:, :], op=ALU.mult),
                      reads=[("H", i, c), ("RSTD",)], writes=[("TH", r)])
                S.add("act", lambda e, c=c, g=g, r=r: e.activation(out=HN[:, c, 2:N + 2], in_=TH[:, r, :], func=AF.Identity, scale=g),
                      reads=[("TH", r), ("VEC",)], writes=[("HN", c)])

        def build_dg(l):
            for c in range(4):
                cw = VEC[:, l * V_LAYER + V_CW + c * CONV_K: l * V_LAYER + V_CW + (c + 1) * CONV_K]
                in0 = IDENT[:, :].unsqueeze(1).broadcast_to([128, CONV_K, 128])
                in1 = cw.unsqueeze(2).broadcast_to([128, CONV_K, 128])
                S.add("dve", lambda e, c=c, in0=in0, in1=in1: e.scalar_tensor_tensor(
                    out=DG[:, c, :, :], in0=in0, scalar=0.5, in1=in1, op0=ALU.mult, op1=ALU.mult),
                    reads=[("IDENT",), ("VEC",)], writes=[("DG", c)], arena=True)

        def front_A_q(s, l, i, q, first=False):
            ub = i % 2
            if first and q == 0:
                S.add("pool", lambda e: e.memset(U[:, 0, :, 0:30], 0.0), writes=[("U", 0, c) for c in range(4)])
                S.add("pool", lambda e: e.memset(P[:, :, 0:16], 0.0), writes=[("P", g) for g in range(4)])
            slot = wload(l, q)
            if q == 3:
                flush_sp()
            for half in range(2):
                b = newbank()
                for k in range(8):
                    S.add("pe", lambda e, b=b, k=k, slot=slot, half=half: e.matmul(
                        PS[:, b, 0:N], WR[:, slot, k * 256 + half * 128: k * 256 + half * 128 + 128], HN[:, k, 2:N + 2],
                        start=(k == 0), stop=(k == 7)),
                        reads=[("WR", slot), ("HN", k)], writes=[("PS", b)])
                if q < 4:
                    c = q
                    if half == 0:
                        S.add("act", lambda e, b=b, c=c: e.activation(out=TH[:, c % 2, :], in_=PS[:, b, 0:N], func=AF.Tanh, scale=0.5),
                              reads=[("PS", b)], writes=[("TH", c % 2)])
                    else:
                        S.add("dve", lambda e, b=b, c=c: e.scalar_tensor_tensor(
                            out=U[:, ub, c, 30:30 + N], in0=TH[:, c % 2, :], scalar=1.0, in1=PS[:, b, 0:N],
                            op0=ALU.add, op1=ALU.mult),
                            reads=[("PS", b), ("TH", c % 2)], writes=[("U", ub, c)])
                else:
                    g = (q - 4) * 2 + half
                    S.add("act", lambda e, b=b, g=g: e.activation(out=P[:, g, 16:16 + N], in_=PS[:, b, 0:N], func=AF.Identity),
                          reads=[("PS", b)], writes=[("P", g)])
            if q == 3:
                S.add("pool", lambda e: e.tensor_copy(out=U[:, 1 - ub, :, 0:30], in_=U[:, ub, :, N:N + 30]),
                      reads=[("U", ub, c) for c in range(4)], writes=[("U", 1 - ub, c) for c in range(4)])

        def front_A(s, l, i, first):
            for q in range(6):
                front_A_q(s, l, i, q, first)

        def conv_ln(s, l, i, hook=None, defer_tail=False):
            ub = i % 2

            cbank = {}

            def conv(c):
                b = newbank()
                cbank[c] = b
                for k in range(CONV_K):
                    S.add("pe", lambda e, b=b, c=c, k=k: e.matmul(PS[:, b, 0:N], DG[:, c, k, :], U[:, ub, c, k:k + N],
                                                                 start=(k == 0), stop=(k == CONV_K - 1)),
                          reads=[("DG", c), ("U", ub, c)], writes=[("PS", b)], arena=True)
                cb = vcol(l, V_CB, c)
                S.add("act", lambda e, b=b, c=c, cb=cb: e.activation(out=CB[:, c, :], in_=PS[:, b, 0:N], func=AF.Identity, bias=cb),
                      reads=[("PS", b), ("VEC",)], writes=[("CB", c)])

            def mean(c):
                b = cbank[c]
                cb = vcol(l, V_CB, c)
                S.add("pe", lambda e, b=b, c=c: e.matmul(PS[:, b, 0:N], NLNB[:, :], CB[:, c, :], start=False, stop=True),
                      reads=[("CB", c), ("NLNB",), ("PS", b)], writes=[("PS", b)])
                S.add("act", lambda e, b=b, c=c, cb=cb: e.activation(out=CF[:, c, :], in_=PS[:, b, 0:N], func=AF.Identity, bias=cb),
                      reads=[("PS", b), ("VEC",)], writes=[("CF", c)])
                S.add("act", lambda e, b=b, c=c, cb=cb: e.activation(out=C2[:, c, :], in_=PS[:, b, 0:N], func=AF.Square, bias=cb),
                      reads=[("PS", b), ("VEC",)], writes=[("C2", c)])

            def var(c):
                b = newbank()
                S.add("pe", lambda e, b=b, c=c: e.matmul(PS[:, b, 0:N], LNB[:, :], C2[:, c, :], start=True, stop=True),
                      reads=[("C2", c), ("LNB",)], writes=[("PS", b)])
                S.add("act", lambda e, b=b, c=c: e.activation(out=RS[:, c % 2, :], in_=PS[:, b, 0:N], func=AF.Ln, bias=EPSV[:, 0:1]),
                      reads=[("PS", b), ("EPSV",)], writes=[("RS", c % 2)])
                S.add("act", lambda e, c=c: e.activation(out=RS[:, c % 2, :], in_=RS[:, c % 2, :], func=AF.Exp, scale=-0.5),
                      reads=[("RS", c % 2)], writes=[("RS", c % 2)])
                S.add("dve", lambda e, c=c: e.tensor_tensor(out=CF[:, c, :], in0=CF[:, c, :], in1=RS[:, c % 2, :], op=ALU.mult),
                      reads=[("CF", c), ("RS", c % 2)], writes=[("CF", c)])

            conv(0)
            conv(1)
            mean(0)
            if hook is not None:
                hook()
            conv(2)
            mean(1)
            var(0)
            conv(3)
            mean(2)
            pool_d(s, l, i)
            var(1)

            def silus():
                for c in range(4):
                    lg, lb = vcol(l, V_LNG, c), vcol(l, V_LNB, c)
                    S.add("act", lambda e, c=c, lg=lg, lb=lb: e.activation(out=Y[:, c, :], in_=CF[:, c, :], func=AF.Silu, bias=lb, scale=lg),
                          reads=[("CF", c), ("VEC",)], writes=[("Y", c)])

            steps = [lambda: mean(3), lambda: (var(2), var(3)), silus]
            if defer_tail:
                return steps
            for f in steps:
                f()
            return []

        pool_cur = {}

        def pool_ops(s, l, i):
            W = N + 16
            A, B = PA4, PB4
            S.add("pool", lambda e: e.tensor_tensor(out=A[:, :, 1:W], in0=P[:, :, 1:W], in1=P[:, :, 0:W - 1], op=ALU.add),
                  reads=[("P", g) for g in range(4)], writes=[("PA", g) for g in range(4)])
            S.add("pool", lambda e: e.tensor_tensor(out=B[:, 0:3, 3:W], in0=A[:, 1:4, 3:W], in1=A[:, 1:4, 1:W - 2], op=ALU.add),
                  reads=[("PA", g) for g in (1, 2, 3)], writes=[("PB", g) for g in (1, 2, 3)])
            S.add("pool", lambda e: e.tensor_tensor(out=A[:, 2:4, 7:W], in0=B[:, 1:3, 7:W], in1=B[:, 1:3, 3:W - 4], op=ALU.add),
                  reads=[("PB", g) for g in (2, 3)], writes=[("PA", g) for g in (2, 3)])
            S.add("pool", lambda e: e.tensor_tensor(out=B[:, 2, 15:W], in0=A[:, 3, 15:W], in1=A[:, 3, 7:W - 8], op=ALU.add),
                  reads=[("PA", 3)], writes=[("PB", 3)])
            if i == 0:
                j0 = 16 + HALO
                for g in range(4):
                    cur, ck, gi = (A, "PA", g) if g in (0, 2) else (B, "PB", g - 1)
                    S.add("pool", lambda e, cur=cur, g=g, gi=gi: e.tensor_tensor(out=PT[:, g, :], in0=cur[:, gi, j0:j0 + 16], in1=PCNT[:, s, g, :], op=ALU.mult),
                          reads=[(ck, g), ("PCNT",)], writes=[("PT", g)])

        def pool_d(s, l, i):
            W = N + 16
            j0 = 16 + HALO
            for g in range(4):
                cur, ck, gi = (PA4, "PA", g) if g in (0, 2) else (PB4, "PB", g - 1)
                inv = 1.0 / float(2 ** (g + 1))
                S.add("dve", lambda e, cur=cur, inv=inv, g=g, gi=gi: e.scalar_tensor_tensor(
                    out=D[:, g, :], in0=cur[:, gi, 16:W], scalar=inv, in1=P[:, g, 16:W], op0=ALU.mult, op1=ALU.subtract),
                    reads=[(ck, g), ("P", g)], writes=[("D", g)])
                if i == 0:
                    S.add("pool", lambda e, g=g: e.tensor_tensor(out=D[:, g, HALO:HALO + 16], in0=PT[:, g, :], in1=P[:, g, j0:j0 + 16], op=ALU.subtract),
                          reads=[("PT", g), ("P", g), ("D", g)], writes=[("D", g)])
            S.add("pool", lambda e: e.tensor_copy(out=P[:, :, 0:16], in_=P[:, :, N:N + 16]),
                  reads=[("P", g) for g in range(4)], writes=[("P", g) for g in range(4)])

        def pool_mm(s, l, i):
            slot = wload(l, 6)
            for g in range(4):
                b = newbank()
                S.add("pe", lambda e, b=b, g=g, slot=slot: e.matmul(PS[:, b, 0:N], WR[:, slot, g * 128:(g + 1) * 128], D[:, g, :], start=True, stop=True),
                      reads=[("WR", slot), ("D", g)], writes=[("PS", b)])
                sc = vcol(l, V_PSC, g)
                S.add("act", lambda e, b=b, g=g, sc=sc: e.activation(out=Y[:, 4 + g, :], in_=PS[:, b, 0:N], func=AF.Identity, scale=sc),
                      reads=[("PS", b), ("VEC",)], writes=[("Y", 4 + g)])

        def back_out(s, l, i, pool_first=False):
            cols = slice(i * N, (i + 1) * N)
            for q in range(4):
                slot = wload(l, 7 + q)
                for half in range(2):
                    m = q * 2 + half
                    b = newbank()
                    korder = [4, 5, 6, 7, 0, 1, 2, 3] if pool_first else [0, 1, 2, 3, 4, 5, 6, 7]
                    for kk, k in enumerate(korder):
                        S.add("pe", lambda e, b=b, k=k, kk=kk, slot=slot, half=half: e.matmul(
                            PS[:, b, 0:N], WR[:, slot, k * 256 + half * 128: k * 256 + half * 128 + 128], Y[:, k, :],
                            start=(kk == 0), stop=(kk == 7)),
                            reads=[("WR", slot), ("Y", k)], writes=[("PS", b)])
                    S.add("dve", lambda e, b=b, m=m: e.tensor_tensor(out=H[:, m, cols], in0=H[:, m, cols], in1=PS[:, b, 0:N], op=ALU.add),
                          reads=[("PS", b), ("H", i, m)], writes=[("H", i, m)])
                    if i == 0:
                        S.add("dve", lambda e, m=m: e.tensor_tensor(out=H[:, m, 0:HALO], in0=H[:, m, 0:HALO], in1=MASK[:, s, :], op=ALU.mult),
                              reads=[("H", i, m), ("MASK",)], writes=[("H", i, m)])

        def ffn_bufs(j):
            sl = (j % 2) * 4
            return (FT[:, sl + 0, :], FT[:, sl + 1, :], FT[:, sl + 2, :], FT[:, sl + 3, :],
                    ("FT", sl), ("FT", sl + 1), ("FT", sl + 2), ("FT", sl + 3))

        def ffn_pair_head(l, j, sk=0):
            W = N - sk
            slot = wload(l, n_mix + j)
            bg, bv = newbank(), newbank()
            for half, b in ((0, bg), (1, bv)):
                for k in range(8):
                    S.add("pe", lambda e, b=b, k=k, slot=slot, half=half: e.matmul(
                        PS[:, b, 0:W + 2], WR[:, slot, k * 256 + half * 128: k * 256 + half * 128 + 128], HN[:, k, sk:N + 2],
                        start=(k == 0), stop=(k == 7)),
                        reads=[("WR", slot), ("HN", k)], writes=[("PS", b)])
            GA, GB, VA, VB, kA, kB, kVA, kVB = ffn_bufs(j)
            GA, GB, VA, VB = GA[:, 0:W], GB[:, 0:W], VA[:, 0:W], VB[:, 0:W]

            def fk(tap, ch):
                cidx = l * V_LAYER + V_FK + tap * 2 * NPAIR + ch
                return VEC[:, cidx:cidx + 1]
            S.add("act", lambda e: e.activation(out=GA, in_=PS[:, bg, 0:W], func=AF.Identity, scale=fk(0, j)),
                  reads=[("PS", bg), ("VEC",)], writes=[kA], arena=True)
            S.add("act", lambda e: e.activation(out=VA, in_=PS[:, bv, 0:W], func=AF.Identity, scale=fk(0, NPAIR + j)),
                  reads=[("PS", bv), ("VEC",)], writes=[kVA], arena=True)
            S.add("dve", lambda e: e.scalar_tensor_tensor(out=GB, in0=PS[:, bg, 1:W + 1], scalar=fk(1, j), in1=GA, op0=ALU.mult, op1=ALU.add),
                  reads=[("PS", bg), kA, ("VEC",)], writes=[kB], arena=True)
            S.add("dve", lambda e: e.scalar_tensor_tensor(out=GA, in0=PS[:, bg, 2:W + 2], scalar=fk(2, j), in1=GB, op0=ALU.mult, op1=ALU.add),
                  reads=[("PS", bg), kB, ("VEC",)], writes=[kA], arena=True)
            S.add("dve", lambda e: e.scalar_tensor_tensor(out=VB, in0=PS[:, bv, 1:W + 1], scalar=fk(1, NPAIR + j), in1=VA, op0=ALU.mult, op1=ALU.add),
                  reads=[("PS", bv), kVA, ("VEC",)], writes=[kVB], arena=True)
            S.add("dve", lambda e: e.scalar_tensor_tensor(out=VA, in0=PS[:, bv, 2:W + 2], scalar=fk(2, NPAIR + j), in1=VB, op0=ALU.mult, op1=ALU.add),
                  reads=[("PS", bv), kVB, ("VEC",)], writes=[kVA], arena=True)

        def ffn_pair_tail(l, j, sk=0):
            W = N - sk
            GA, GB, VA, VB, kA, kB, kVA, kVB = ffn_bufs(j)
            GA, GB, VA, VB = GA[:, 0:W], GB[:, 0:W], VA[:, 0:W], VB[:, 0:W]
            S.add("act", lambda e: e.activation(out=GB, in_=GA, func=AF.Silu),
                  reads=[kA], writes=[kB], arena=True)
            S.add("pool", lambda e: e.tensor_tensor(out=ACTV[:, j, 0:W], in0=VA, in1=GB, op=ALU.mult),
                  reads=[kVA, kB], writes=[("ACTV", j)], arena=True)

        def ffn_down(l, i, m, sk=0):
            W = N - sk
            cols = slice(i * N + sk, (i + 1) * N)
            slot = wload(l, n_mix + NPAIR + m)
            b = newbank()
            for j in range(NPAIR):
                S.add("pe", lambda e, b=b, j=j, slot=slot: e.matmul(PS[:, b, 0:W], WR[:, slot, j * 128:(j + 1) * 128], ACTV[:, j, 0:W],
                                                                   start=(j == 0), stop=(j == NPAIR - 1)),
                      reads=[("WR", slot), ("ACTV", j)], writes=[("PS", b)], arena=True)
            S.add("dve", lambda e, b=b, m=m: e.tensor_tensor(out=H[:, m, cols], in0=H[:, m, cols], in1=PS[:, b, 0:W], op=ALU.add),
                  reads=[("PS", b), ("H", i, m)], writes=[("H", i, m)])

        def ffn_down_first2(l, i, sk=0):
            W = N - sk
            cols = slice(i * N + sk, (i + 1) * N)
            slots = [wload(l, n_mix + NPAIR + m) for m in (0, 1)]
            banks = [newbank(), newbank()]
            for m in (0, 1):
                for j in range(NPAIR - 1):
                    S.add("pe", lambda e, b=banks[m], j=j, slot=slots[m]: e.matmul(
                        PS[:, b, 0:W], WR[:, slot, j * 128:(j + 1) * 128], ACTV[:, j, 0:W], start=(j == 0), stop=False),
                        reads=[("WR", slots[m]), ("ACTV", j)], writes=[("PS", banks[m])], arena=True)
            j = NPAIR - 1
            for m in (0, 1):
                S.add("pe", lambda e, b=banks[m], slot=slots[m]: e.matmul(
                    PS[:, b, 0:W], WR[:, slot, j * 128:(j + 1) * 128], ACTV[:, j, 0:W], start=False, stop=True),
                    reads=[("WR", slots[m]), ("ACTV", j)], writes=[("PS", banks[m])], arena=True)
                S.add("dve", lambda e, b=banks[m], m=m: e.tensor_tensor(out=H[:, m, cols], in0=H[:, m, cols], in1=PS[:, b, 0:W], op=ALU.add),
                      reads=[("PS", banks[m]), ("H", i, m)], writes=[("H", i, m)])

        def final_tile(s, i):
            cols = slice(i * N, (i + 1) * N)
            for c in range(8):
                S.add("act", lambda e, c=c: e.activation(out=SQ[:, c, :], in_=H[:, c, cols], func=AF.Square),
                      reads=[("H", i, c)], writes=[("SQ", c)])
            b = newbank()
            for c in range(8):
                S.add("pe", lambda e, c=c, b=b: e.matmul(PS[:, b, 0:N], ONES[:, :], SQ[:, c, :], start=(c == 0), stop=(c == 7)),
                      reads=[("SQ", c), ("ONES",)], writes=[("PS", b)])
            S.add("act", lambda e, b=b: e.activation(out=RSTD[:, :], in_=PS[:, b, 0:N], func=AF.Ln, bias=EPSV[:, 0:1]),
                  reads=[("PS", b), ("EPSV",)], writes=[("RSTD",)])
            S.add("act", lambda e: e.activation(out=RSTD[:, :], in_=RSTD[:, :], func=AF.Exp, scale=-0.5),
                  reads=[("RSTD",)], writes=[("RSTD",)])
            for c in range(8):
                fg = VEC[:, DEPTH * V_LAYER + c: DEPTH * V_LAYER + c + 1]
                S.add("pool", lambda e, c=c: e.tensor_tensor(out=H[:, c, cols], in0=H[:, c, cols], in1=RSTD[:, :], op=ALU.mult),
                      reads=[("H", i, c), ("RSTD",)], writes=[("H", i, c)])
                S.add("act", lambda e, c=c, fg=fg: e.activation(out=H[:, c, cols], in_=H[:, c, cols], func=AF.Identity, scale=fg),
                      reads=[("H", i, c), ("VEC",)], writes=[("H", i, c)])
            lo = i * N
            hi = (i + 1) * N
            dst = out_d[s, i, :, :].rearrange("p (c t) -> p c t", c=8)
            cnt = s + 1

            def do_store(dst=dst, lo=lo, hi=hi, i=i, cnt=cnt):
                S.add("act", lambda e: e.dma_start(out=dst, in_=H[:, :, lo:hi]),
                      reads=[("H", i, c) for c in range(8)], dma=(s_out[i], 16 * cnt))
            pending_sp.append(do_store)

        unloaded = set()
        pending_sp = []
        pending_ld = []

        def flush_sp(final=False):
            lds = list(pending_ld)
            del pending_ld[:]
            for f in lds:
                f()
            while pending_sp:
                pending_sp.pop(0)()
            if final:
                lds = list(pending_ld)
                del pending_ld[:]
                for f in lds:
                    f()

        load_x_tile(0, 0)
        cast_some(n_mix)
        prep(0, 0, V_N1G, None)
        first_loads = [True]
        NPRE = 0
        for s in range(NSEG):
            for l in range(DEPTH):
                last = (l == DEPTH - 1)
                S.arena_handoff()
                front_A(s, l, 0, first=True)
                if first_loads[0]:
                    first_loads[0] = False
                    for i in range(1, NT):
                        load_x_tile(0, i)
                build_dg(l)
                for i in range(NT):
                    if i + 1 < NT:
                        prep(i + 1, l, V_N1G, None, defer_hn=True)
                        hook = (lambda i=i, l=l: prep_hn(i + 1, l, V_N1G, None))
                    else:
                        prep(0, l, V_N2G, "zero", defer_hn=True)
                        hook = (lambda l=l: prep_hn(0, l, V_N2G, "zero"))
                    if i == 0:
                        pool_ops(s, l, 0)
                    steps = conv_ln(s, l, i, hook, defer_tail=True)
                    if s == 0 and l == 0:
                        cast_some((n_ffn + NT - 1) // NT)
                    if i + 1 < NT:
                        for q in range(4):
                            front_A_q(s, l, i + 1, q)
                            if q < len(steps):
                                steps[q]()
                        pool_mm(s, l, i)
                        front_A_q(s, l, i + 1, 4)
                        front_A_q(s, l, i + 1, 5)
                    else:
                        S.arena_handoff()
                        pool_mm(s, l, i)
                        for f in steps:
                            f()
                    back_out(s, l, i, pool_first=(i + 1 >= NT))
                    if i + 1 < NT:
                        pool_ops(s, l, i + 1)
                for i in range(NT):
                    sk = max(0, 64 - 32 * (DEPTH - 1 - l)) if i == 0 else 0
                    for j in range(NPRE if i == 0 else 0, NPAIR):
                        ffn_pair_head(l, j, sk)
                        if j >= 1:
                            ffn_pair_tail(l, j - 1, sk)
                        if j == 8:
                            flush_sp()
                    ffn_pair_tail(l, NPAIR - 1, sk)
                    if s == 0 and l == 0:
                        cast_some(((DEPTH - 1) * (n_mix + n_ffn) + NT - 1) // NT)
                    ffn_down_first2(l, i, sk)
                    for m in range(2, 4):
                        ffn_down(l, i, m, sk)
                    if i + 1 < NT:
                        prep(i + 1, l, V_N2G, "copy")
                    elif not last:
                        prep(0, l + 1, V_N1G, None)
                    elif s + 1 < NSEG:
                        flush_sp(final=(NT < 3))
                        prep(0, 0, V_N1G, None)
                    for m in range(4, 8):
                        ffn_down(l, i, m, sk)
                    if last:
                        final_tile(s, i)
                        if s + 1 < NSEG:
                            unloaded.add(i)
                            pending_sp.append(lambda s=s, i=i: pending_ld.append(lambda: (unloaded.discard(i), load_x_tile(s + 1, i))))
        flush_sp(final=True)
        assert not cast_queue
        S.add("sp", lambda e: e.nop(), writes=[("H", i, c) for i in range(NT) for c in range(8)])

        S.emit_all(nc,
                   {"pe": block.tensor, "act": block.scalar, "dve": block.vector, "pool": block.gpsimd, "sp": block.sync},
                   {"pe": s_pe, "act": s_act, "dve": s_dve, "pool": s_pool})
    return nc


def _layout_weights(l, w_in, pool_w, w_out, w_up, w_down):
    parts = []
    W = w_in[l]
    blocks = []
    for c in range(4):
        blocks.append(W[:, 512 + c * 128: 512 + (c + 1) * 128])
        blocks.append(W[:, c * 128:(c + 1) * 128])
    for g in range(4):
        blocks.append(W[:, 1024 + g * 128: 1024 + (g + 1) * 128])
    Wp = np.concatenate(blocks, axis=1)
    parts.append(Wp.reshape(8, 128, 6, 256).transpose(2, 1, 0, 3).reshape(-1))
    parts.append(pool_w[l].transpose(1, 0, 2).reshape(-1))
    parts.append(w_out[l].reshape(8, 128, 4, 256).transpose(2, 1, 0, 3).reshape(-1))
    Wu = w_up[l]
    up = np.concatenate([Wu[:, :D_FF].reshape(8, 128, NPAIR, 1, 128), Wu[:, D_FF:].reshape(8, 128, NPAIR, 1, 128)], axis=3)
    parts.append(up.transpose(2, 1, 0, 3, 4).reshape(-1))
    parts.append(w_down[l].reshape(NPAIR, 128, 8, 128).transpose(2, 1, 0, 3).reshape(-1))
    flat = np.concatenate(parts).astype(np.float32, copy=False)
    assert flat.size == 128 * LAYER_ELEMS
    return flat


def _layout_vecs(DEPTH, norm1_g, conv_dw_k, conv_dw_b, conv_ln_g, conv_ln_b, pool_scale, norm2_g, ffn_dw_k, final_g):
    v = np.zeros((128, DEPTH * V_LAYER + 8), np.float32)
    for l in range(DEPTH):
        o = l * V_LAYER
        v[:, o + V_N1G:o + V_N1G + 8] = norm1_g[l].reshape(8, 128).T
        v[:, o + V_N2G:o + V_N2G + 8] = norm2_g[l].reshape(8, 128).T
        v[:, o + V_CB:o + V_CB + 4] = conv_dw_b[l].reshape(4, 128).T
        v[:, o + V_LNG:o + V_LNG + 4] = conv_ln_g[l].reshape(4, 128).T
        v[:, o + V_LNB:o + V_LNB + 4] = conv_ln_b[l].reshape(4, 128).T
        v[:, o + V_PSC:o + V_PSC + 4] = pool_scale[l].reshape(4, 128).T
        v[:, o + V_CW:o + V_CW + 4 * CONV_K] = conv_dw_k[l].reshape(CONV_K, 4, 128).transpose(2, 1, 0).reshape(128, 4 * CONV_K)
        v[:, o + V_FK:o + V_FK + 3 * 2 * NPAIR] = ffn_dw_k[l].reshape(3, 2 * NPAIR, 128).transpose(2, 0, 1).reshape(128, 3 * 2 * NPAIR)
    v[:, DEPTH * V_LAYER:DEPTH * V_LAYER + 8] = final_g.reshape(8, 128).T
    return v


def _consts():
    c = np.zeros((128, 512), np.float32)
    c[:, 0:128] = np.eye(128, dtype=np.float32)
    c[:, 128:256] = 1.0 / 1024.0
    blk = np.zeros((128, 128), np.float32)
    blk[:64, :64] = 1.0 / 64.0
    blk[64:, 64:] = 1.0 / 64.0
    c[:, 256:384] = blk
    c[:, 384:512] = -blk
    return c


_PROG_CACHE = {}


def run_model(x, meta_tokens, norm1_g, w_in, conv_dw_k, conv_dw_b, conv_ln_g, conv_ln_b, pool_w, pool_scale,
              w_out, norm2_g, w_up, ffn_dw_k, w_down, final_g, NT, NSEG, NCORES, NQ, DEPTH):
    x = np.asarray(x, np.float32)
    B, SEQ, _ = x.shape
    TSEG = NT * N
    OWN = TSEG - HALO
    L = SEQ + N_META
    assert L == NQ * OWN and B * NQ == NCORES * NSEG
    args = [np.asarray(a, np.float32) for a in (norm1_g, w_in, conv_dw_k, conv_dw_b, conv_ln_g, conv_ln_b, pool_w,
                                                 pool_scale, w_out, norm2_g, w_up, ffn_dw_k, w_down, final_g)]
    norm1_g, w_in, conv_dw_k, conv_dw_b, conv_ln_g, conv_ln_b, pool_w, pool_scale, w_out, norm2_g, w_up, ffn_dw_k, w_down, final_g = args
    wf = np.stack([_layout_weights(l, w_in, pool_w, w_out, w_up, w_down) for l in range(DEPTH)], 0)
    vecs = _layout_vecs(DEPTH, norm1_g, conv_dw_k, conv_dw_b, conv_ln_g, conv_ln_b, pool_scale, norm2_g, ffn_dw_k, final_g)
    consts = _consts()
    meta = np.asarray(meta_tokens, np.float32)
    in_maps = []
    for core in range(NCORES):
        xin = np.zeros((NSEG, NT, 128, 8 * N), np.float32)
        mask = np.ones((128, NSEG, HALO), np.float32)
        pcnt = np.zeros((128, NSEG, 4, 16), np.float32)
        for s in range(NSEG):
            q = core * NSEG + s
            b, r = divmod(q, NQ)
            p0 = r * OWN - HALO
            seg = np.zeros((TSEG, D_MODEL), np.float32)
            lo = max(p0, 0)
            hi = p0 + TSEG
            hcat_rows = []
            if lo < N_META:
                hcat_rows.append(meta[lo:min(hi, N_META)])
            if hi > N_META:
                hcat_rows.append(x[b, max(lo, N_META) - N_META: hi - N_META])
            seg[lo - p0:] = np.concatenate(hcat_rows, 0)
            xin[s] = seg.T.reshape(8, 128, NT, N).transpose(2, 1, 0, 3).reshape(NT, 128, 8 * N)
            for g in range(4):
                w = 2 ** (g + 1)
                if r == 0:
                    mask[:, s] = 0.0
                    pcnt[:, s, g, :] = 1.0 / np.minimum(np.arange(16) + 1, w).astype(np.float32)
                else:
                    pcnt[:, s, g, :] = 1.0 / w
        in_maps.append({"xin": xin, "wf": wf, "vecs": vecs, "consts": consts,
                        "mask": mask.reshape(128, -1), "pcnt": pcnt.reshape(128, -1)})
    key = (NT, NSEG, DEPTH)
    if key not in _PROG_CACHE:
        _PROG_CACHE[key] = build_program(NT, NSEG, DEPTH)
    nc = _PROG_CACHE[key]
    res = run_bass_kernel_spmd(nc, in_maps, core_ids=list(range(NCORES)))
    y = np.empty((B, SEQ, D_MODEL), np.float32)
    for core in range(NCORES):
        o = res.results[core]["out"]
        for s in range(NSEG):
            q = core * NSEG + s
            b, r = divmod(q, NQ)
            full = o[s].reshape(NT, 128, 8, N).transpose(2, 1, 0, 3).reshape(D_MODEL, TSEG).T
            tok = full[HALO:]
            p_lo = r * OWN
            if r == 0:
                y[b, 0:OWN - N_META] = tok[N_META:]
            else:
                y[b, p_lo - N_META:p_lo - N_META + OWN] = tok
    return y


def kernel(x, meta_tokens, norm1_g, w_in, conv_dw_k, conv_dw_b, conv_ln_g, conv_ln_b, pool_w, pool_scale,
           w_out, norm2_g, w_up, ffn_dw_k, w_down, final_g):
    return run_model(x, meta_tokens, norm1_g, w_in, conv_dw_k, conv_dw_b, conv_ln_g, conv_ln_b, pool_w, pool_scale,
                     w_out, norm2_g, w_up, ffn_dw_k, w_down, final_g, NT=5, NSEG=2, NCORES=8, NQ=4, DEPTH=2)
```
